# Optimizing a Trainium2 kernel written in Bass

```python
import jax, jax.numpy as jnp
from jax import lax
import numpy as np

D_MODEL = 1024
BATCH = 16
SEQ = 2048
DEPTH = 2

GRID_W = 64
CTX_LEN = 256
Q_BLOCK = 128
ROPE_THETA = 10000.0
EPS = 1e-6
N_EVEN = (DEPTH + 1) // 2
N_ODD = DEPTH // 2
MIX_HALF = D_MODEL // 2

A_HEAD_DIM = 64
A_Q_HEADS = MIX_HALF // A_HEAD_DIM
A_KV_HEADS = 2
A_GROUP = A_Q_HEADS // A_KV_HEADS
B_GROUPS = 8
B_WIDTH = MIX_HALF
B_GROUP_DIM = B_WIDTH // B_GROUPS
B_CHUNK = 128
C_HEADS = 8
C_NOPE = 64
C_ROPE = 32
C_V = MIX_HALF // C_HEADS
C_Q_RANK = D_MODEL // 4
C_KV_RANK = D_MODEL // 8
D_WIDTH = MIX_HALF
D_CONV = 31
FF_DIM = 4 * D_MODEL
N_MOD = 6

EV_Q = A_Q_HEADS * A_HEAD_DIM
EV_KV = A_KV_HEADS * A_HEAD_DIM
EV_IN = EV_Q + 2 * EV_KV + 2 * B_WIDTH
OD_IN = C_Q_RANK + C_KV_RANK + C_ROPE + 2 * D_WIDTH

kernel_name = "hybrid_gqa_gmlp_mla_conformer_prefix_dit"


def rms_norm(x, g):
    xf = x.astype(jnp.float32)
    y = xf * lax.rsqrt(jnp.mean(xf * xf, axis=-1, keepdims=True) + EPS)
    return (y * g.astype(jnp.float32)).astype(x.dtype)


def layer_norm(x, g, b):
    xf = x.astype(jnp.float32)
    mu = jnp.mean(xf, axis=-1, keepdims=True)
    var = jnp.mean(jnp.square(xf - mu), axis=-1, keepdims=True)
    y = (xf - mu) * lax.rsqrt(var + EPS)
    return (y * g.astype(jnp.float32) + b.astype(jnp.float32)).astype(x.dtype)


def modulate(x, g, shift, scale):
    return rms_norm(x, g) * (1 + scale) + shift


def axial_angles(length, d_rot):
    rows = length // GRID_W
    row = jnp.broadcast_to(jnp.arange(rows)[:, None], (rows, GRID_W)).reshape(-1).astype(jnp.float32)
    col = jnp.broadcast_to(jnp.arange(GRID_W)[None, :], (rows, GRID_W)).reshape(-1).astype(jnp.float32)
    d_axis = d_rot // 2
    inv = ROPE_THETA ** (-jnp.arange(0, d_axis, 2, dtype=jnp.float32) / d_axis)
    return jnp.concatenate([row[:, None] * inv, col[:, None] * inv], axis=-1)


def apply_rope(x, ang):
    d = x.shape[-1]
    xf = x.astype(jnp.float32).reshape(x.shape[:-1] + (d // 2, 2))
    cos, sin = jnp.cos(ang), jnp.sin(ang)
    x0, x1 = xf[..., 0], xf[..., 1]
    out = jnp.stack([x0 * cos - x1 * sin, x0 * sin + x1 * cos], axis=-1)
    return out.reshape(x.shape).astype(x.dtype)


def to_heads(t, n_heads, head_dim):
    b, l, _ = t.shape
    return t.reshape(b, l, n_heads, head_dim).transpose(0, 2, 1, 3)


def from_heads(o):
    b, n, l, hd = o.shape
    return o.transpose(0, 2, 1, 3).reshape(b, l, n * hd)


def block_attention(q, k, v):
    b, hk, g, lq, dk = q.shape
    scale = dk ** -0.5
    qb = jnp.moveaxis(q.reshape(b, hk, g, lq // Q_BLOCK, Q_BLOCK, dk), 3, 0)

    def one_block(qi):
        s = jnp.einsum("bhgqd,bhkd->bhgqk", qi, k, preferred_element_type=jnp.float32) * scale
        p = jax.nn.softmax(s, axis=-1)
        return jnp.einsum("bhgqk,bhkd->bhgqd", p.astype(v.dtype), v)

    o = lax.map(one_block, qb)
    return jnp.moveaxis(o, 0, 3).reshape(b, hk, g, lq, v.shape[-1])


def spatial_gating(z, norm_g, w_s, b_s):
    b, l, _ = z.shape
    u, v = jnp.split(jax.nn.gelu(z), 2, axis=-1)
    v = rms_norm(v.reshape(b, l, B_GROUPS, B_GROUP_DIM), norm_g)
    v = v.reshape(b, l // B_CHUNK, B_CHUNK, B_GROUPS, B_GROUP_DIM)
    sv = jnp.einsum("gpq,bnqgc->bnpgc", w_s, v) + b_s.T[None, None, :, :, None]
    return u * sv.reshape(b, l, B_WIDTH)


def even_mixer(h_lat, h_ctx, need_ctx, w_in, q_norm_g, k_norm_g, sgu_norm_g, sgu_w, sgu_b):
    cuts = [EV_Q, EV_Q + EV_KV, EV_Q + 2 * EV_KV]
    w_q, w_k, w_v, w_z = jnp.split(w_in, cuts, axis=1)

    def gqa_q(qp, ang):
        b, l, _ = qp.shape
        q = rms_norm(to_heads(qp, A_Q_HEADS, A_HEAD_DIM), q_norm_g)
        if ang is not None:
            q = apply_rope(q, ang)
        return q.reshape(b, A_KV_HEADS, A_GROUP, l, A_HEAD_DIM)

    def gqa_kv(kp, vp, ang):
        k = rms_norm(to_heads(kp, A_KV_HEADS, A_HEAD_DIM), k_norm_g)
        if ang is not None:
            k = apply_rope(k, ang)
        return k, to_heads(vp, A_KV_HEADS, A_HEAD_DIM)

    def merge(o):
        b, hk, g, l, d = o.shape
        return from_heads(o.reshape(b, hk * g, l, d))

    b, l, _ = h_lat.shape
    ang = axial_angles(l, A_HEAD_DIM)
    qp, kp, vp, zp = jnp.split(h_lat @ w_in, cuts, axis=-1)
    kc, vc = gqa_kv(h_ctx @ w_k, h_ctx @ w_v, None)
    kl, vl = gqa_kv(kp, vp, ang)
    o_att = block_attention(gqa_q(qp, ang), jnp.concatenate([kc, kl], axis=2),
                            jnp.concatenate([vc, vl], axis=2))
    out_lat = jnp.concatenate([merge(o_att), spatial_gating(zp, sgu_norm_g, sgu_w, sgu_b)], axis=-1)
    out_ctx = None
    if need_ctx:
        oc = block_attention(gqa_q(h_ctx @ w_q, None), kc, vc)
        out_ctx = jnp.concatenate([merge(oc), spatial_gating(h_ctx @ w_z, sgu_norm_g, sgu_w, sgu_b)],
                                  axis=-1)
    return out_lat, out_ctx


def odd_mixer(h_lat, h_ctx, need_ctx, w_in, q_norm_g, kv_norm_g, w_uq, w_ukv, conv_w, conv_b,
              ln_g, ln_b):
    cuts = [C_Q_RANK, C_Q_RANK + C_KV_RANK, C_Q_RANK + C_KV_RANK + C_ROPE]
    w_cq, w_ckv, w_kr, w_cv = jnp.split(w_in, cuts, axis=1)

    def mla_q(cq, ang):
        b, l, _ = cq.shape
        q = to_heads(rms_norm(cq, q_norm_g) @ w_uq, C_HEADS, C_NOPE + C_ROPE)
        qn, qr = jnp.split(q, [C_NOPE], axis=-1)
        if ang is not None:
            qr = apply_rope(qr, ang)
        return jnp.concatenate([qn, qr], axis=-1)[:, :, None]

    def mla_kv(ckv, kr, ang):
        b, l, _ = ckv.shape
        kv = to_heads(rms_norm(ckv, kv_norm_g) @ w_ukv, C_HEADS, C_NOPE + C_V)
        kn, v = jnp.split(kv, [C_NOPE], axis=-1)
        kr = kr[:, None]
        if ang is not None:
            kr = apply_rope(kr, ang)
        k = jnp.concatenate([kn, jnp.broadcast_to(kr, (b, C_HEADS, l, C_ROPE))], axis=-1)
        return k, v

    def conformer(z):
        a, gt = jnp.split(z, 2, axis=-1)
        y = a * jax.nn.sigmoid(gt)
        y = lax.conv_general_dilated(y, conv_w[:, None, :], window_strides=(1,),
                                     padding=[(D_CONV // 2, D_CONV // 2)],
                                     dimension_numbers=("NWC", "WIO", "NWC"),
                                     feature_group_count=D_WIDTH) + conv_b
        return jax.nn.silu(layer_norm(y, ln_g, ln_b))

    b, l, _ = h_lat.shape
    ang = axial_angles(l, C_ROPE)
    cq, ckv, kr, zc = jnp.split(h_lat @ w_in, cuts, axis=-1)
    kc, vc = mla_kv(h_ctx @ w_ckv, h_ctx @ w_kr, None)
    kl, vl = mla_kv(ckv, kr, ang)
    o_att = block_attention(mla_q(cq, ang), jnp.concatenate([kc, kl], axis=2),
                            jnp.concatenate([vc, vl], axis=2))[:, :, 0]
    out_lat = jnp.concatenate([from_heads(o_att), conformer(zc)], axis=-1)
    out_ctx = None
    if need_ctx:
        oc = block_attention(mla_q(h_ctx @ w_cq, None), kc, vc)[:, :, 0]
        out_ctx = jnp.concatenate([from_heads(oc), conformer(h_ctx @ w_cv)], axis=-1)
    return out_lat, out_ctx


def sq_relu_mlp(h, w1, w2):
    return jnp.square(jax.nn.relu(h @ w1)) @ w2


def setup_inputs(seed: int = 0) -> dict:
    key = jax.random.key(seed)
    ks = jax.random.split(key, 32)

    def nrm(k, shape, scale=1.0):
        return jax.random.normal(k, shape, jnp.float32) * scale

    def gain(k, shape):
        return 1.0 + 0.05 * jax.random.normal(k, shape, jnp.float32)

    D = D_MODEL
    return {
        "x": nrm(ks[0], (BATCH, SEQ, D)),
        "c": nrm(ks[1], (BATCH, D)),
        "ctx": nrm(ks[2], (BATCH, CTX_LEN, D)),
        "c_ctx": nrm(ks[3], (D,)),
        "ada_w": nrm(ks[4], (DEPTH, D, N_MOD * D), 0.5 * D ** -0.5),
        "ada_b": nrm(ks[5], (DEPTH, N_MOD * D), 0.02),
        "norm1_g": gain(ks[6], (DEPTH, D)),
        "norm2_g": gain(ks[7], (DEPTH, D)),
        "w_out": nrm(ks[8], (DEPTH, D, D), D ** -0.5),
        "mlp_w1": nrm(ks[9], (DEPTH, D, FF_DIM), D ** -0.5),
        "mlp_w2": nrm(ks[10], (DEPTH, FF_DIM, D), FF_DIM ** -0.5),
        "ev_w_in": nrm(ks[11], (N_EVEN, D, EV_IN), D ** -0.5),
        "ev_q_norm_g": gain(ks[12], (N_EVEN, A_HEAD_DIM)),
        "ev_k_norm_g": gain(ks[13], (N_EVEN, A_HEAD_DIM)),
        "ev_sgu_norm_g": gain(ks[14], (N_EVEN, B_GROUPS, B_GROUP_DIM)),
        "ev_sgu_w": nrm(ks[15], (N_EVEN, B_GROUPS, B_CHUNK, B_CHUNK), B_CHUNK ** -0.5),
        "ev_sgu_b": gain(ks[16], (N_EVEN, B_GROUPS, B_CHUNK)),
        "od_w_in": nrm(ks[17], (N_ODD, D, OD_IN), D ** -0.5),
        "od_q_norm_g": gain(ks[18], (N_ODD, C_Q_RANK)),
        "od_kv_norm_g": gain(ks[19], (N_ODD, C_KV_RANK)),
        "od_w_uq": nrm(ks[20], (N_ODD, C_Q_RANK, C_HEADS * (C_NOPE + C_ROPE)), C_Q_RANK ** -0.5),
        "od_w_ukv": nrm(ks[21], (N_ODD, C_KV_RANK, C_HEADS * (C_NOPE + C_V)), C_KV_RANK ** -0.5),
        "od_conv_w": nrm(ks[22], (N_ODD, D_CONV, D_WIDTH), D_CONV ** -0.5),
        "od_conv_b": nrm(ks[23], (N_ODD, D_WIDTH), 0.02),
        "od_ln_g": gain(ks[24], (N_ODD, D_WIDTH)),
        "od_ln_b": nrm(ks[25], (N_ODD, D_WIDTH), 0.02),
        "final_g": gain(ks[26], (D,)),
    }


def reference(x, c, ctx, c_ctx, ada_w, ada_b, norm1_g, norm2_g, w_out, mlp_w1, mlp_w2,
              ev_w_in, ev_q_norm_g, ev_k_norm_g, ev_sgu_norm_g, ev_sgu_w, ev_sgu_b,
              od_w_in, od_q_norm_g, od_kv_norm_g, od_w_uq, od_w_ukv, od_conv_w, od_conv_b,
              od_ln_g, od_ln_b, final_g):
    x_lat, x_ctx = x, ctx
    silu_c = jax.nn.silu(c)
    silu_cc = jax.nn.silu(c_ctx)
    for i in range(DEPTH):
        last = i == DEPTH - 1
        j = i // 2
        m = jnp.split(silu_c @ ada_w[i] + ada_b[i], N_MOD, axis=-1)
        sh1, sc1, g1, sh2, sc2, g2 = [t[:, None, :] for t in m]
        sh1c, sc1c, g1c, sh2c, sc2c, g2c = jnp.split(silu_cc @ ada_w[i] + ada_b[i], N_MOD, axis=-1)

        h_lat = modulate(x_lat, norm1_g[i], sh1, sc1)
        h_ctx = modulate(x_ctx, norm1_g[i], sh1c, sc1c)
        if i % 2 == 0:
            o_lat, o_ctx = even_mixer(h_lat, h_ctx, not last, ev_w_in[j], ev_q_norm_g[j],
                                      ev_k_norm_g[j], ev_sgu_norm_g[j], ev_sgu_w[j], ev_sgu_b[j])
        else:
            o_lat, o_ctx = odd_mixer(h_lat, h_ctx, not last, od_w_in[j], od_q_norm_g[j],
                                     od_kv_norm_g[j], od_w_uq[j], od_w_ukv[j], od_conv_w[j],
                                     od_conv_b[j], od_ln_g[j], od_ln_b[j])

        x_lat = x_lat + g1 * (o_lat @ w_out[i])
        x_lat = x_lat + g2 * sq_relu_mlp(modulate(x_lat, norm2_g[i], sh2, sc2), mlp_w1[i], mlp_w2[i])
        if not last:
            x_ctx = x_ctx + g1c * (o_ctx @ w_out[i])
            x_ctx = x_ctx + g2c * sq_relu_mlp(modulate(x_ctx, norm2_g[i], sh2c, sc2c),
                                              mlp_w1[i], mlp_w2[i])
    return rms_norm(x_lat, final_g)
```

```python
import numpy as np
from contextlib import ExitStack
import concourse.bass as bass
import concourse.mybir as mybir
from concourse.bass_utils import run_bass_kernel_spmd

F32 = mybir.dt.float32
BF16 = mybir.dt.bfloat16
AF = mybir.ActivationFunctionType
ALU = mybir.AluOpType
AX = mybir.AxisListType

EPS = 1e-6
PREFETCH_D = True
MERGE_FILL = True
NT = 4608
NLAT = 4096
NTILE = 9


class Buf:
    __slots__ = ("name", "w", "r", "excl")

    def __init__(self, name=""):
        self.name = name
        self.w = None
        self.r = []
        self.excl = False


class Op:
    __slots__ = ("eng", "fn", "dma", "deps", "sig", "sigidx", "chan", "chanval", "cost", "alldeps", "pos", "nobar", "tag", "tbl")

    def __init__(self, eng, fn, dma, deps, cost=300.0):
        self.eng = eng
        self.fn = fn
        self.dma = dma
        self.deps = deps
        self.sig = False
        self.sigidx = 0
        self.chan = None
        self.chanval = 0
        self.cost = cost
        self.alldeps = None
        self.pos = 0
        self.nobar = False
        self.tbl = None


ENGS = ["pe", "act", "dve", "pool", "sp"]
NCHAN = {"sp": 20, "pool": 10, "act": 4}
SCHED_WINDOW = 48
HOP_NS = 150.0
ACT_TABLE_NS = 1300.0


class Prog:
    def __init__(self):
        self.ops = []
        self.chan_rr = {q: 0 for q in NCHAN}
        self.chan_last = {}
        self.chan_cnt = {}
        self.last_eng = {}
        self.bar = {}

    def add(self, eng, fn, reads=(), writes=(), dma=False, cost=300.0, nobar=False):
        i = len(self.ops)
        deps = set()
        for b in reads:
            if b.w is not None:
                deps.add(b.w)
            if b.excl:
                for j in b.r:
                    if self.ops[j].eng != eng:
                        deps.add(j)
        for b in writes:
            if b.w is not None:
                deps.add(b.w)
            deps.update(b.r)
        for b in reads:
            b.r.append(i)
        for b in writes:
            b.w = i
            b.r = []
        deps.discard(i)
        op = Op(eng, fn, dma, deps, cost)
        op.nobar = nobar
        op.tag = getattr(self, "curtag", "")
        if dma:
            c = self.chan_rr[eng]
            self.chan_rr[eng] = (c + 1) % NCHAN[eng]
            key = (eng, c)
            if key in self.chan_last:
                op.deps.add(self.chan_last[key])
            self.chan_last[key] = i
            self.chan_cnt[key] = self.chan_cnt.get(key, 0) + 1
            op.chan = key
            op.chanval = 16 * self.chan_cnt[key]
        op.alldeps = set(op.deps)
        if eng in self.bar:
            op.alldeps.add(self.bar[eng])
        self.last_eng[eng] = i
        self.ops.append(op)
        return i

    def pe(self, fn, r=(), w=(), cost=300.0):
        return self.add("pe", fn, r, w, cost=cost)

    def act(self, fn, r=(), w=(), cost=300.0):
        return self.add("act", fn, r, w, cost=cost)

    def dve(self, fn, r=(), w=(), cost=300.0):
        return self.add("dve", fn, r, w, cost=cost)

    def pool(self, fn, r=(), w=(), cost=300.0):
        return self.add("pool", fn, r, w, cost=cost)

    def dma(self, q, fn, r=(), w=(), cost=4000.0, nobar=False):
        return self.add(q, fn, r, w, dma=True, cost=cost, nobar=nobar)

    def barrier(self):
        deps = set(self.last_eng.values())
        for key, i in self.chan_last.items():
            j = i
            if not self.ops[j].nobar:
                deps.add(j)
        for e in ENGS:
            i = len(self.ops)
            op = Op(e, lambda eng: eng.nop(), False, set(deps), 30.0)
            op.alldeps = set(deps)
            if e in self.bar:
                op.alldeps.add(self.bar[e])
            self.ops.append(op)
            self.last_eng[e] = i
            self.bar[e] = i

    def schedule(self):
        import bisect
        ops = self.ops
        n = len(ops)
        users = [[] for _ in range(n)]
        indeg = [0] * n
        for i, op in enumerate(ops):
            indeg[i] = len(op.alldeps)
            for d in op.alldeps:
                users[d].append(i)
        level = [0.0] * n
        for i in range(n - 1, -1, -1):
            m = 0.0
            for u in users[i]:
                if level[u] > m:
                    m = level[u]
            level[i] = ops[i].cost + m
        avail = {e: [] for e in ENGS}
        ready_t = [0.0] * n
        finish = [0.0] * n
        for i, op in enumerate(ops):
            if indeg[i] == 0:
                avail[op.eng].append(i)
        eng_free = {e: 0.0 for e in ENGS}
        order = {e: [] for e in ENGS}
        done = 0
        cur_tbl = None
        while done < n:
            best = None
            for e in ENGS:
                av = avail[e]
                if not av:
                    continue
                ef = eng_free[e]
                cb = None
                for i in av[:SCHED_WINDOW]:
                    st = ready_t[i] if ready_t[i] > ef else ef
                    if e == "act" and ops[i].tbl is not None and ops[i].tbl != cur_tbl:
                        st += ACT_TABLE_NS
                    key = (st, -level[i], i)
                    if cb is None or key < cb:
                        cb = key
                if best is None or cb < best[0]:
                    best = (cb, e)
            (st, _lv, i), e = best
            op = ops[i]
            avail[e].remove(i)
            if e == "act" and op.tbl is not None:
                cur_tbl = op.tbl
            if op.dma:
                eng_free[e] = st + 60.0
            else:
                eng_free[e] = st + op.cost
            finish[i] = st + op.cost
            op.pos = len(order[e])
            order[e].append(i)
            done += 1
            for u in users[i]:
                indeg[u] -= 1
                t = finish[i] + (HOP_NS if ops[u].eng != e or op.dma else 0.0)
                if t > ready_t[u]:
                    ready_t[u] = t
                if indeg[u] == 0:
                    bisect.insort(avail[ops[u].eng], u)
        self.est_ns = max(finish) if n else 0.0
        self.finish = finish
        self.order = order
        return order

    def emit(self, nc, stack):
        ops = self.ops
        order = self.schedule()
        seen = {e: {} for e in ENGS}
        for e in ENGS:
            sd = seen[e]
            for i in order[e]:
                op = ops[i]
                red = {}
                for d in op.deps:
                    p = ops[d]
                    if p.dma:
                        key = ("c", p.chan)
                        val = p.chanval
                    else:
                        if p.eng == e and e == "pe":
                            continue
                        key = ("e", p.eng)
                        val = p.pos
                    if key not in red or val > red[key][0]:
                        red[key] = (val, d)
                keep = []
                for key, (val, d) in red.items():
                    if sd.get(key, -1) >= val:
                        continue
                    sd[key] = val
                    keep.append(d)
                op.deps = keep
        for op in ops:
            for d in op.deps:
                p = ops[d]
                if not p.dma:
                    p.sig = True
        cnt = {e: 0 for e in ENGS}
        for e in ENGS:
            for i in order[e]:
                op = ops[i]
                if op.sig:
                    cnt[e] += 1
                    op.sigidx = cnt[e]
        esem = {e: stack.enter_context(nc.semaphore("s_" + e)) for e in ENGS}
        csem = {}
        for key in self.chan_cnt:
            csem[key] = stack.enter_context(nc.semaphore("c_%s%d" % key))
        block = stack.enter_context(nc.Block())
        handles = {"pe": block.tensor, "act": block.scalar, "dve": block.vector,
                   "pool": block.gpsimd, "sp": block.sync}
        stats = {"sig": dict(cnt), "est_us": self.est_ns / 1e3}
        for e in ENGS:
            my = [ops[i] for i in order[e]]
            stats[e] = len(my)
            if not my:
                continue

            def body(eng, my=my, e=e):
                nw = 0
                for op in my:
                    for d in op.deps:
                        p = ops[d]
                        if p.dma:
                            eng.wait_ge(csem[p.chan], p.chanval)
                        else:
                            eng.wait_ge(esem[p.eng], p.sigidx)
                        nw += 1
                    ins = op.fn(eng)
                    if op.dma:
                        ins.then_inc(csem[op.chan], 16)
                    elif op.sig:
                        ins.then_inc(esem[e], 1)
                stats[e + "_waits"] = nw

            handles[e](body)
        return stats


class T:
    __slots__ = ("ap", "b")

    def __init__(self, ap, name=""):
        self.ap = ap
        self.b = Buf(name)


class Arena:
    def __init__(self, t, cap):
        self.t = t
        self.cap = cap
        self.off = 0

    def f32(self, n, name=""):
        assert self.off % 4 == 0
        a = self.t[:, self.off // 4: self.off // 4 + n]
        self.off += n * 4
        assert self.off <= self.cap, ("SBUF overflow", self.off, name)
        return T(a, name)

    def bf16(self, n, name=""):
        n2 = (n + 1) // 2
        a = self.t[:, self.off // 4: self.off // 4 + n2].bitcast(BF16)[:, 0:n]
        self.off += n2 * 4
        assert self.off <= self.cap, ("SBUF overflow", self.off, name)
        return T(a, name)


class Rot:
    def __init__(self, items):
        self.items = items
        self.i = 0

    def next(self):
        x = self.items[self.i % len(self.items)]
        self.i += 1
        return x


ACT_TBL = {AF.Sqrt: "sqrt", AF.Gelu_apprx_tanh: "gelu", AF.Exp: "exp", AF.Sigmoid: "sigmoid", AF.Silu: "silu", AF.Ln: "exp"}


def row_of(t):
    return 0 if t < 4 else (1 if t < 8 else 2)


def build(debug_dump=None, upto=99):
    nc = bass.Bass("TRN2", target_bir_lowering=False)
    P = Prog()

    def din(name, shape, dt=F32):
        return nc.dram_tensor(name, list(shape), dt, kind="ExternalInput").ap()

    def dscr(name, shape, dt):
        return nc.dram_tensor(name, list(shape), dt, kind="Internal").ap()

    xT_in = din("xT_in", [8, 128, NT])
    cT_in = din("cT", [128, 8, 3])
    ada_w = din("ada_w", [2, 6, 128, 8, 1024])
    ada_b = din("ada_b", [2, 128, 48])
    n1g = din("n1g", [2, 128, 8])
    n2g = din("n2g", [2, 128, 8])
    fing = din("fing", [128, 8])
    w_out = din("w_out", [2, 128, 8, 1024])
    w1 = din("w1", [2, 128, 8, 4096])
    w2 = din("w2", [2, 128, 32, 1024])
    ev_w_in = din("ev_w_in", [128, 8, 1792])
    ev_gq = din("ev_gq", [128, 2, 64])
    ev_gk = din("ev_gk", [128, 2, 64])
    ev_cs = din("ev_cs", [128, 2, 17, 64])
    ev_sgug = din("ev_sgug", [128, 512])
    ev_wsT = din("ev_wsT", [128, 8, 128])
    ev_bsT = din("ev_bsT", [128, 8])
    od_w_in = din("od_w_in", [128, 8, 1440])
    od_gq = din("od_gq", [128, 256])
    od_gkv = din("od_gkv", [128, 128])
    od_wuq = din("od_wuq", [128, 2, 768])
    od_wukv = din("od_wukv", [128, 1024])
    od_cs = din("od_cs", [128, 2, 17, 32])
    od_cw = din("od_cw", [128, 4, 31])
    od_vec = din("od_vec", [128, 3, 4])
    outT = nc.dram_tensor("outT", [8, 128, NLAT], F32, kind="ExternalOutput").ap()

    xT = dscr("xT_s", [8, 128, NT], F32)
    oT = dscr("oT_s", [8, 128, NT], BF16)
    QT0 = dscr("QT0_s", [4, 128, NT], BF16)
    KT0 = dscr("KT0_s", [2, 128, NT], BF16)
    VA0 = dscr("VA0_s", [36, 128, 512], BF16)
    QT1 = dscr("QT1_s", [8, 96, NT], BF16)
    KT1 = dscr("KT1_s", [8, 96, NT], BF16)
    VA1 = dscr("VA1_s", [36, 128, 1024], BF16)
    YT = dscr("YT_s", [4, 128, NLAT], BF16)
    b_xT = [Buf() for _ in range(NTILE)]
    b_oTa = [Buf() for _ in range(NTILE)]
    b_oTb = [Buf() for _ in range(NTILE)]
    b_QT = [Buf() for _ in range(NTILE)]
    b_KT = [Buf() for _ in range(NTILE)]
    b_VA = [Buf() for _ in range(NTILE)]
    b_YT = [Buf() for _ in range(NTILE)]
    b_out = Buf()

    with ExitStack() as st:
        CAP = 200 * 1024
        arena_t = st.enter_context(nc.sbuf_tensor("arena", [128, CAP // 4], F32))
        A = Arena(arena_t, CAP)
        psbig = [st.enter_context(nc.psum_tensor("psw%d" % i, [128, 1024], F32)) for i in range(4)]
        psb = [T(psbig[i // 2][:, (i % 2) * 512:(i % 2 + 1) * 512], "ps%d" % i) for i in range(8)]
        for p_ in psb:
            p_.b.excl = True
        PS = Rot(psb[0:7])

        def ps_bf(p):
            return p.ap.bitcast(BF16)

        ident_f = A.f32(128, "identf")
        ident_b = A.bf16(128, "identb")
        ones_b = A.bf16(128, "onesb")
        ones_f = A.f32(128, "onesf")
        eps_t = A.f32(1, "eps")
        scT = A.f32(24, "scT")
        modT = A.f32(2 * 144, "modT")
        A1t = A.f32(2 * 24, "A1")
        A2t = A.f32(2 * 24, "A2")
        gn1 = A.f32(16, "gn1")
        gn2 = A.f32(16, "gn2")
        gfin = A.f32(8, "gfin")
        adab = A.f32(96, "adab")
        bconst = Buf("const")
        P.pool(lambda e: e.memset(ident_f.ap, 1.0), w=[bconst])
        P.pool(lambda e: e.affine_select(out=ident_f.ap, in_=ident_f.ap, pattern=[[-1, 128]],
                                         compare_op=ALU.is_equal, fill=0.0, base=0, channel_multiplier=1),
               r=[bconst], w=[bconst])
        P.dve(lambda e: e.tensor_copy(out=ident_b.ap, in_=ident_f.ap), r=[bconst], w=[bconst])
        P.pool(lambda e: e.memset(ones_b.ap, 1.0), w=[bconst])
        P.pool(lambda e: e.memset(ones_f.ap, 1.0), w=[bconst])
        P.pool(lambda e: e.memset(eps_t.ap, EPS), w=[bconst])
        P.dma("sp", lambda e: e.dma_start(out=scT.ap, in_=cT_in.rearrange("p c r -> p (c r)")), w=[scT.b])
        P.act(lambda e: e.activation(out=scT.ap, in_=scT.ap, func=AF.Silu), r=[scT.b], w=[scT.b])
        P.dma("sp", lambda e: e.dma_start(out=gn1.ap.rearrange("p (l c) -> p l c", l=2), in_=n1g.rearrange("l p c -> p l c")), w=[bconst])
        P.dma("sp", lambda e: e.dma_start(out=gn2.ap.rearrange("p (l c) -> p l c", l=2), in_=n2g.rearrange("l p c -> p l c")), w=[bconst])
        P.dma("sp", lambda e: e.dma_start(out=gfin.ap, in_=fing), w=[bconst])
        P.dma("sp", lambda e: e.dma_start(out=adab.ap.rearrange("p (l c) -> p l c", l=2), in_=ada_b.rearrange("l p c -> p l c")), w=[bconst])
        BASE = A.off

        mod3 = A1v = A2v = None

        def set_layer(L):
            nonlocal mod3, A1v, A2v
            mod3 = modT.ap[:, L * 144:(L + 1) * 144].rearrange("p (m r) -> p m r", r=3)
            A1v = A1t.ap[:, L * 24:(L + 1) * 24].rearrange("p (c r) -> p c r", r=3)
            A2v = A2t.ap[:, L * 24:(L + 1) * 24].rearrange("p (c r) -> p c r", r=3)
        set_layer(0)
        scT3 = scT.ap.rearrange("p (c r) -> p c r", r=3)

        def SH1(c, row):
            return mod3[:, 0 + c, row:row + 1]

        def G1(c, row):
            return mod3[:, 16 + c, row:row + 1]

        def SH2(c, row):
            return mod3[:, 24 + c, row:row + 1]

        def G2(c, row):
            return mod3[:, 40 + c, row:row + 1]

        def fsz(ap):
            n = 1
            for d in ap.shape[1:]:
                n *= int(d)
            return n

        def MM(out, lhsT, rhs, start, stop, r, w):
            n = fsz(rhs)
            c = 4.0 * max(n / 2.4, 107.0) if rhs.dtype == F32 else max(n / 2.4 + 5.0, 64.0)
            P.pe(lambda e: e.matmul(out, lhsT=lhsT, rhs=rhs, start=start, stop=stop), r, w, cost=c)

        def TR(out, in_, r, w):
            P.pe(lambda e: e.transpose(out=out, in_=in_, identity=ident_b.ap), list(r) + [bconst], w, cost=110.0)

        def ACT(out, in_, func, r, w, bias=None, scale=None):
            kw = {}
            if bias is not None:
                kw["bias"] = bias
            if scale is not None:
                kw["scale"] = scale
            i = P.act(lambda e: e.activation(out=out, in_=in_, func=func, **kw), r, w, cost=100.0 + fsz(out) / 1.2)
            P.ops[i].tbl = ACT_TBL.get(func)

        def ecost(eng, n, mult=1.0):
            return (70.0 + n / 0.96 * mult) if eng == "dve" else (100.0 + n / 0.55)

        def TT(eng, out, in0, in1, op, r, w):
            P.add(eng, lambda e: e.tensor_tensor(out=out, in0=in0, in1=in1, op=op), r, w, cost=ecost(eng, fsz(out)))

        def STT(out, in0, scalar, in1, op0, op1, r, w):
            P.dve(lambda e: e.scalar_tensor_tensor(out=out, in0=in0, scalar=scalar, in1=in1, op0=op0, op1=op1), r, w,
                  cost=ecost("dve", fsz(out)))

        def TS(eng, out, in0, scalar1, op0, r, w):
            P.add(eng, lambda e: e.tensor_scalar(out=out, in0=in0, scalar1=scalar1, scalar2=None, op0=op0), r, w,
                  cost=ecost(eng, fsz(out)))

        def RED(out, in_, r, w):
            P.dve(lambda e: e.tensor_reduce(out=out, in_=in_, axis=AX.X, op=ALU.add), r, w, cost=ecost("dve", fsz(in_)))

        def RECIP(out, in_, r, w):
            P.dve(lambda e: e.reciprocal(out=out, in_=in_), r, w, cost=ecost("dve", fsz(out), 8.0))

        def COPY(eng, out, in_, r, w):
            P.add(eng, lambda e: e.tensor_copy(out=out, in_=in_), r, w, cost=ecost(eng, fsz(out)))

        def DMA(q, out, in_, r, w, nobar=False):
            nbytes = 128 * fsz(out) * (2 if out.dtype == BF16 else 4)
            P.dma(q, lambda e: e.dma_start(out=out, in_=in_), r, w, cost=2000.0 + nbytes / 200.0, nobar=nobar)

        def MEMSET(ap, val, w):
            P.pool(lambda e: e.memset(ap, val), (), w, cost=ecost("pool", fsz(ap)))

        def cols(t):
            return slice(t * 512, (t + 1) * 512)

        def v3(ap, c):
            return ap.rearrange("p (c t) -> p c t", c=c)

        def load_x(xt, src, t, q="sp"):
            DMA(q, v3(xt.ap, 8), src[:, :, cols(t)].rearrange("c p t -> p c t"), [b_xT[t]], [xt.b])

        def store_x(xt, dst, t, dstbuf, q="sp"):
            DMA(q, dst[:, :, cols(t)].rearrange("c p t -> p c t"), v3(xt.ap, 8), [xt.b], [dstbuf])

        def rms_stats(xt, sq, rs, sqbufs=None):
            sb_ = [sq.b] if sqbufs is None else sqbufs
            ACT(sq.ap, xt.ap, AF.Square, [xt.b], sb_)
            p = psb[7]
            for c in range(8):
                MM(p.ap, ones_b.ap, sq.ap[:, c * 512:(c + 1) * 512], c == 0, c == 7, sb_ + [bconst], [p.b])
            ACT(rs.ap, p.ap, AF.Sqrt, [p.b, bconst], [rs.b], bias=eps_t.ap[:, 0:1], scale=1.0 / 1024)
            RECIP(rs.ap, rs.ap, [rs.b], [rs.b])

        def modulate(xt, rs, Av, SHf, row, ht, tmps):
            for c in range(8):
                tm = tmps.next()
                STT(tm.ap, xt.ap[:, c * 512:(c + 1) * 512], Av[:, c, row:row + 1], rs.ap, ALU.mult, ALU.mult,
                    [xt.b, rs.b, modT.b], [tm.b])
                ACT(ht.ap[:, c * 512:(c + 1) * 512], tm.ap, AF.Identity, [tm.b, modT.b], [ht.b], bias=SHf(c, row), scale=1.0)

        def small_rstd(ss, n, inv_d):
            ACT(ss.ap[:, 0:n], ss.ap[:, 0:n], AF.Sqrt, [ss.b, bconst], [ss.b], bias=eps_t.ap[:, 0:1], scale=inv_d)
            RECIP(ss.ap[:, 0:n], ss.ap[:, 0:n], [ss.b], [ss.b])

        def cast_load(dst_ap, src_ap, wbuf, pieces=1):
            if pieces == 1:
                DMA("pool", dst_ap, src_ap, [], [wbuf])
                return
            n = dst_ap.shape[1]
            step = n // pieces
            for i in range(pieces):
                DMA("pool", dst_ap[:, i * step:(i + 1) * step], src_ap[:, i * step:(i + 1) * step], [], [wbuf])

        def mk(n, nm, f32=True, k=2):
            return Rot([(A.f32(n, nm + str(i)) if f32 else A.bf16(n, nm + str(i))) for i in range(k)])

        def rope(pT, pap, nh, hd, C3, S3, tabbufs, ta, tb, out_ap, outT_, eng_add="pool"):
            n = nh * hd
            hh = hd // 2
            p3 = pap.rearrange("p (h d) -> p h d", h=nh)
            TT("dve", ta.ap[:, 0:n].rearrange("p (h d) -> p h d", h=nh), p3, C3.unsqueeze(1).to_broadcast([128, nh, hd]),
               ALU.mult, [pT.b] + tabbufs, [ta.b])
            p4 = pap.rearrange("p (h i two) -> p h i two", h=nh, two=2)
            S4 = S3.rearrange("p (i two) -> p i two", two=2)
            tb4 = tb.ap[:, 0:n].rearrange("p (h i two) -> p h i two", h=nh, two=2)
            for a in range(2):
                TT("dve", tb4[:, :, :, a], p4[:, :, :, 1 - a], S4[:, :, a].unsqueeze(1).to_broadcast([128, nh, hh]),
                   ALU.mult, [pT.b] + tabbufs, [tb.b])
            TT(eng_add, out_ap, ta.ap[:, 0:n].rearrange("p (h d) -> p h d", h=nh),
               tb.ap[:, 0:n].rearrange("p (h d) -> p h d", h=nh), ALU.add, [ta.b, tb.b], [outT_.b])

        def phase_M(layers):
            A.off = BASE
            wts = [A.f32(8 * 1024, "adaw%d" % i) for i in range(2)]
            mrow = A.f32(6144, "mrow")
            k = 0
            for L in layers:
                set_layer(L)
                for pc in range(6):
                    wt = wts[k % 2]
                    k += 1
                    DMA("sp", wt.ap, ada_w[L, pc].rearrange("p c n -> p (c n)"), [], [wt.b])
                    for h in range(2):
                        p = PS.next()
                        for c in range(8):
                            MM(p.ap[0:3, :], scT3[:, c, :], wt.ap[:, c * 1024 + h * 512:c * 1024 + h * 512 + 512], c == 0, c == 7,
                               [wt.b, scT.b], [p.b])
                        COPY("dve", mrow.ap[0:3, (pc * 2 + h) * 512:(pc * 2 + h + 1) * 512], p.ap[0:3, :], [p.b], [mrow.b])
                pm = PS.next()
                for m in range(48):
                    P.pe(lambda e, m=m, pm=pm: e.transpose(out=pm.ap[:, m * 3:m * 3 + 3], in_=mrow.ap[0:3, m * 128:(m + 1) * 128],
                                                           identity=ident_f.ap[0:3, 0:3]), [mrow.b, bconst], [pm.b], cost=110.0)
                adab3 = adab.ap.rearrange("p (l c) -> p l c", l=2)
                TT("dve", mod3, pm.ap[:, 0:144].rearrange("p (m r) -> p m r", r=3),
                   adab3[:, L, :].unsqueeze(2).to_broadcast([128, 48, 3]), ALU.add, [pm.b, bconst], [modT.b])
                g1 = gn1.ap.rearrange("p (l c) -> p l c", l=2)[:, L, :].unsqueeze(2).to_broadcast([128, 8, 3])
                g2 = gn2.ap.rearrange("p (l c) -> p l c", l=2)[:, L, :].unsqueeze(2).to_broadcast([128, 8, 3])
                STT(A1v, mod3[:, 8:16, :], 1.0, g1, ALU.add, ALU.mult, [modT.b, bconst], [modT.b])
                STT(A2v, mod3[:, 32:40, :], 1.0, g2, ALU.add, ALU.mult, [modT.b, bconst], [modT.b])
            P.barrier()

        def phase_A0(xsrc):
            A.off = BASE
            w = A.bf16(8 * 1792, "w_in0")
            w3 = v3(w.ap, 8)
            cast_load(w3, ev_w_in, w.b, pieces=8)
            gq = A.f32(128, "gq")
            gk = A.f32(128, "gk")
            DMA("sp", gq.ap, ev_gq.rearrange("p a d -> p (a d)"), [], [gq.b])
            DMA("sp", gk.ap, ev_gk.rearrange("p a d -> p (a d)"), [], [gk.b])
            sgug = A.f32(512, "sgug")
            wsT = A.bf16(8 * 128, "wsT")
            bsT = A.f32(8, "bsT")
            DMA("sp", sgug.ap, ev_sgug, [], [sgug.b])
            cast_load(wsT.ap, ev_wsT.rearrange("p g q -> p (g q)"), wsT.b)
            DMA("sp", bsT.ap, ev_bsT, [], [bsT.b])
            tab = {}
            tabs = []
            for nm in ("q", "k"):
                tC = A.f32(17 * 64, "C" + nm)
                tS = A.f32(17 * 64, "S" + nm)
                tab[nm] = (tC, tS)
            xt = A.f32(4096, "x")
            cs4 = xt.ap[:, 0:2 * 17 * 64].rearrange("p (a b d) -> p a b d", a=2, b=17)
            DMA("sp", xt.ap[:, 0:2 * 17 * 64], ev_cs.rearrange("p a b d -> p (a b d)"), [], [xt.b])
            for nm, g in (("q", gq), ("k", gk)):
                tC, tS = tab[nm]
                g3 = g.ap.rearrange("p (a d) -> p a d", a=2)
                TT("dve", tC.ap.rearrange("p (b d) -> p b d", b=17), cs4[:, 0],
                   g3[:, 0, :].unsqueeze(1).to_broadcast([128, 17, 64]), ALU.mult, [xt.b, g.b], [tC.b])
                TT("dve", tS.ap.rearrange("p (b d) -> p b d", b=17), cs4[:, 1],
                   g3[:, 1, :].unsqueeze(1).to_broadcast([128, 17, 64]), ALU.mult, [xt.b, g.b], [tS.b])
            sq = A.bf16(4096, "sq")
            rs = A.f32(512, "rs")
            hts = Rot([A.bf16(4096, "h%d" % i) for i in range(2)])
            tmps = Rot([A.f32(512, "tm%d" % i) for i in range(3)])
            QTst = Rot([A.bf16(4 * 512, "QTst%d" % i) for i in range(2)])
            KTst = Rot([A.bf16(2 * 512, "KTst%d" % i) for i in range(2)])
            VAl = [A.bf16(4 * 512, "VAst%d" % i) for i in range(2)]
            for v in VAl:
                MEMSET(v.ap, 1.0, [v.b])
            VAst = Rot(VAl)
            OGst = Rot([A.bf16(4 * 512, "OGst%d" % i) for i in range(2)])
            sqq = mk(512, "sqq", k=1)
            ssq = mk(8, "ssq")
            t1 = mk(512, "t1", k=1)
            t2 = mk(512, "t2", k=1)
            qrot = mk(512, "qrot", f32=False)
            sqk = mk(128, "sqk")
            ssk = mk(8, "ssk")
            t1k = mk(128, "t1k")
            t2k = mk(128, "t2k")
            kdup = mk(256, "kdup", f32=False)
            uu = mk(512, "u")
            vg = mk(512, "vg")
            sqv = mk(512, "sqv", k=1)
            ssv = mk(8, "ssv")
            vnb = mk(512, "vnb", f32=False)
            tsv = mk(512, "tsv", k=1)
            og = mk(512, "og", f32=False)

            def rope_norm(pT, pap, nh, nm, blk, sqt, sst, ta, tb, outs):
                n = nh * 64
                C, S = tab[nm]
                ACT(sqt.ap[:, 0:n], pap, AF.Square, [pT.b], [sqt.b])
                RED(sst.ap[:, 0:nh], sqt.ap[:, 0:n].rearrange("p (h d) -> p h d", h=nh), [sqt.b], [sst.b])
                small_rstd(sst, nh, 1.0 / 64)
                C3 = C.ap.rearrange("p (b d) -> p b d", b=17)[:, blk, :]
                S3 = S.ap.rearrange("p (b d) -> p b d", b=17)[:, blk, :]
                rope(pT, pap, nh, 64, C3, S3, [C.b, S.b], ta, tb, ta.ap[:, 0:n].rearrange("p (h d) -> p h d", h=nh), ta)
                for (oT_, oap) in outs:
                    TT("dve", oap, ta.ap[:, 0:n].rearrange("p (h d) -> p h d", h=nh),
                       sst.ap[:, 0:nh].unsqueeze(2).to_broadcast([128, nh, 64]), ALU.mult, [ta.b, sst.b], [oT_.b])

            order = [8] + list(range(8))
            load_x(xt, xsrc, order[0])
            for ti, t in enumerate(order):
                row = row_of(t)
                rms_stats(xt, sq, rs)
                ht = hts.next()
                modulate(xt, rs, A1v, SH1, row, ht, tmps)
                if ti + 1 < len(order):
                    load_x(xt, xsrc, order[ti + 1])
                qst, kst, vst, ost = QTst.next(), KTst.next(), VAst.next(), OGst.next()
                for s in range(4):
                    blk = 16 if t == 8 else (t % 4) * 4 + s
                    sl = slice(s * 128, (s + 1) * 128)
                    pQ, pKV, pZU, pZV = PS.next(), PS.next(), PS.next(), PS.next()
                    groups = [(pQ, 0, 512), (pKV, 512, 256), (pZU, 768, 512), (pZV, 1280, 512)]
                    for c in range(8):
                        for (pp, c0, nn) in groups:
                            MM(pp.ap[:, 0:nn], ht.ap[:, c * 512 + s * 128:c * 512 + s * 128 + 128], w3[:, c, c0:c0 + nn],
                               c == 0, c == 7, [ht.b, w.b], [pp.b])
                    qr = qrot.next()
                    rope_norm(pQ, pQ.ap, 8, "q", blk, sqq.next(), ssq.next(), t1.next(), t2.next(),
                              [(qr, qr.ap.rearrange("p (h d) -> p h d", h=8))])
                    pT = PS.next()
                    pTb = ps_bf(pT)
                    for c in range(4):
                        TR(pTb[:, c * 128:(c + 1) * 128], qr.ap[:, c * 128:(c + 1) * 128], [qr.b], [pT.b])
                    ACT(v3(qst.ap, 4)[:, :, sl], v3(pTb[:, 0:512], 4), AF.Copy, [pT.b], [qst.b])
                    kd = kdup.next()
                    kd4 = kd.ap.rearrange("p (k d e) -> p k d e", k=2, d=2)
                    rope_norm(pKV, pKV.ap[:, 0:128], 2, "k", blk, sqk.next(), ssk.next(), t1k.next(), t2k.next(),
                              [(kd, kd4[:, :, 0, :]), (kd, kd4[:, :, 1, :])])
                    pT2 = PS.next()
                    pT2b = ps_bf(pT2)
                    for k in range(2):
                        TR(pT2b[:, k * 128:(k + 1) * 128], kd.ap[:, k * 128:(k + 1) * 128], [kd.b], [pT2.b])
                    ACT(v3(kst.ap, 2)[:, :, sl], v3(pT2b[:, 0:256], 2), AF.Copy, [pT2.b], [kst.b])
                    v5 = vst.ap.rearrange("p (s k a d) -> p s k a d", s=4, k=2, a=2)
                    vin = pKV.ap[:, 128:256].rearrange("p (k d) -> p k d", k=2)
                    ACT(v5[:, s, :, 0, 0:64], vin, AF.Copy, [pKV.b], [vst.b])
                    ACT(v5[:, s, :, 1, 64:128], vin, AF.Copy, [pKV.b], [vst.b])
                    u_, vg_, sqv_, ssv_, vnb_, tsv_, og_ = (uu.next(), vg.next(), sqv.next(), ssv.next(), vnb.next(),
                                                            tsv.next(), og.next())
                    ACT(u_.ap, pZU.ap, AF.Gelu_apprx_tanh, [pZU.b], [u_.b])
                    ACT(vg_.ap, pZV.ap, AF.Gelu_apprx_tanh, [pZV.b], [vg_.b])
                    TT("pool", sqv_.ap, vg_.ap, vg_.ap, ALU.mult, [vg_.b], [sqv_.b])
                    RED(ssv_.ap, sqv_.ap.rearrange("p (g d) -> p g d", g=8), [sqv_.b], [ssv_.b])
                    small_rstd(ssv_, 8, 1.0 / 64)
                    TT("dve", vg_.ap.rearrange("p (g d) -> p g d", g=8), vg_.ap.rearrange("p (g d) -> p g d", g=8),
                       ssv_.ap.unsqueeze(2).to_broadcast([128, 8, 64]), ALU.mult, [vg_.b, ssv_.b], [vg_.b])
                    TT("pool", vnb_.ap, vg_.ap, sgug.ap, ALU.mult, [vg_.b, sgug.b], [vnb_.b])
                    pS = PS.next()
                    for g in range(8):
                        MM(pS.ap[:, g * 64:(g + 1) * 64], wsT.ap[:, g * 128:(g + 1) * 128], vnb_.ap[:, g * 64:(g + 1) * 64],
                           True, True, [vnb_.b, wsT.b], [pS.b])
                    TT("dve", tsv_.ap.rearrange("p (g d) -> p g d", g=8), pS.ap.rearrange("p (g d) -> p g d", g=8),
                       bsT.ap.unsqueeze(2).to_broadcast([128, 8, 64]), ALU.add, [pS.b, bsT.b], [tsv_.b])
                    TT("pool", og_.ap, tsv_.ap, u_.ap, ALU.mult, [tsv_.b, u_.b], [og_.b])
                    pT3 = PS.next()
                    pT3b = ps_bf(pT3)
                    for c in range(4):
                        TR(pT3b[:, c * 128:(c + 1) * 128], og_.ap[:, c * 128:(c + 1) * 128], [og_.b], [pT3.b])
                    COPY("dve", v3(ost.ap, 4)[:, :, sl], v3(pT3b[:, 0:512], 4), [pT3.b], [ost.b])
                DMA("sp", QT0[:, :, cols(t)].rearrange("c p t -> p c t"), v3(qst.ap, 4), [qst.b], [b_QT[t]])
                DMA("sp", KT0[:, :, cols(t)].rearrange("c p t -> p c t"), v3(kst.ap, 2), [kst.b], [b_KT[t]])
                DMA("sp", VA0[t * 4:(t + 1) * 4].rearrange("s p f -> p s f"), v3(vst.ap, 4), [vst.b], [b_VA[t]])
                DMA("sp", oT[4:8, :, cols(t)].rearrange("c p t -> p c t"), v3(ost.ap, 4), [ost.b], [b_oTb[t]])
            P.barrier()

        def phase_B(layer, filler=None):
            A.off = BASE
            if layer == 0:
                KP, nkt, vw, scale = 128, 2, 512, 64 ** -0.5
                KTd, QTd, VAd = KT0, QT0, VA0
            else:
                KP, nkt, vw, scale = 96, 8, 1024, 96 ** -0.5
                KTd, QTd, VAd = KT1, QT1, VA1
            kt = A.bf16(nkt * 2304, "kt")
            kt3 = v3(kt.ap, nkt)
            va = A.bf16(18 * vw, "va")
            va3 = v3(va.ap, 18)
            qts = Rot([A.bf16(1024, "qt%d" % i) for i in range(3)])
            pp_ = Rot([A.bf16(1024, "Pp%d" % i) for i in range(3)])
            rec = Rot([A.f32(512, "rec%d" % i) for i in range(2)])
            ost = Rot([A.bf16(512, "ost%d" % i) for i in range(2)])
            Sbanks = Rot([(psb[0], psb[1], psbig[0]), (psb[2], psb[3], psbig[1])])
            if filler is None:
                Abanks = Rot([(psb[4], psb[5]), (psb[6], psb[7])])
            else:
                Abanks = Rot([(psb[4], psb[5])])
                cpn = Rot([A.f32(512, "cpn%d" % i) for i in range(2)])
                cpd = Rot([A.f32(512, "cpd%d" % i) for i in range(2)])
                PS.items = [psb[6], psb[7]]
                fst = filler[0]()
                if PREFETCH_D and layer == 0:
                    assert A.off <= W1_OFF, A.off

            def load_q(job):
                (hp, q0, nq, nkb, tq) = job
                qt = qts.next()
                if layer == 0:
                    DMA("sp", qt.ap[:, 0:nq], QTd[hp, :, q0:q0 + nq], [b_QT[tq]], [qt.b])
                else:
                    DMA("sp", v3(qt.ap, 2)[0:96, :, 0:nq], QTd[2 * hp:2 * hp + 2, :, q0:q0 + nq].rearrange("h p t -> p h t"),
                        [b_QT[tq]], [qt.b])
                return qt

            def run_job(job, qt):
                (hp, q0, nq, nkb, tq) = job
                if layer == 0:
                    kv = hp // 2
                    q_e, q_o = qt.ap[0:64, 0:nq], qt.ap[64:128, 0:nq]
                    k_e = lambda kb: kt3[0:64, kv, kb * 128:(kb + 1) * 128]
                    k_o = lambda kb: kt3[64:128, kv, kb * 128:(kb + 1) * 128]
                    v_e = lambda kb: va3[:, kb, (2 * kv) * 128:(2 * kv + 1) * 128]
                    v_o = lambda kb: va3[:, kb, (2 * kv + 1) * 128:(2 * kv + 2) * 128]
                else:
                    q3 = v3(qt.ap, 2)
                    q_e, q_o = q3[0:96, 0, 0:nq], q3[0:96, 1, 0:nq]
                    k_e = lambda kb: kt3[0:96, 2 * hp, kb * 128:(kb + 1) * 128]
                    k_o = lambda kb: kt3[0:96, 2 * hp + 1, kb * 128:(kb + 1) * 128]
                    v_e = lambda kb: va3[:, kb, (2 * hp) * 128:(2 * hp + 1) * 128]
                    v_o = lambda kb: va3[:, kb, (2 * hp + 1) * 128:(2 * hp + 2) * 128]
                aE, aO = Abanks.next()

                def issue_S(kb):
                    sE, sO, sBig = Sbanks.next()
                    MM(sE.ap[:, 0:nq], k_e(kb), q_e, True, True, [kt.b, qt.b], [sE.b])
                    MM(sO.ap[:, 0:nq], k_o(kb), q_o, True, True, [kt.b, qt.b], [sO.b])
                    return sE, sO, sBig
                cur = issue_S(0)
                for kb in range(nkb):
                    nxt = issue_S(kb + 1) if kb + 1 < nkb else None
                    sE, sO, sBig = cur
                    pp = pp_.next()
                    if nq == 512:
                        ACT(pp.ap, sBig[:, 0:1024], AF.Exp, [sE.b, sO.b], [pp.b], scale=scale)
                    else:
                        ACT(v3(pp.ap, 2)[:, :, 0:nq], v3(sBig[:, 0:1024], 2)[:, :, 0:nq], AF.Exp, [sE.b, sO.b], [pp.b], scale=scale)
                    MM(aE.ap[:, 0:nq], v_e(kb), pp.ap[:, 0:nq], kb == 0, kb == nkb - 1, [va.b, pp.b], [aE.b])
                    MM(aO.ap[:, 0:nq], v_o(kb), pp.ap[:, 512:512 + nq], kb == 0, kb == nkb - 1, [va.b, pp.b], [aO.b])
                    cur = nxt
                rc, os_ = rec.next(), ost.next()
                if filler is None:
                    RECIP(rc.ap[64:128, 0:nq], aE.ap[64:128, 0:nq], [aE.b], [rc.b])
                    RECIP(rc.ap[0:64, 0:nq], aO.ap[0:64, 0:nq], [aO.b], [rc.b])
                    TT("dve", os_.ap[0:64, 0:nq], aE.ap[0:64, 0:nq], rc.ap[64:128, 0:nq], ALU.mult, [aE.b, rc.b], [os_.b])
                    TT("dve", os_.ap[64:128, 0:nq], aO.ap[64:128, 0:nq], rc.ap[0:64, 0:nq], ALU.mult, [aO.b, rc.b], [os_.b])
                else:
                    cn, cd = cpn.next(), cpd.next()
                    COPY("dve", cn.ap[0:64, 0:nq], aE.ap[0:64, 0:nq], [aE.b], [cn.b])
                    COPY("dve", cd.ap[0:64, 0:nq], aE.ap[64:128, 0:nq], [aE.b], [cd.b])
                    COPY("dve", cn.ap[64:128, 0:nq], aO.ap[64:128, 0:nq], [aO.b], [cn.b])
                    COPY("dve", cd.ap[64:128, 0:nq], aO.ap[0:64, 0:nq], [aO.b], [cd.b])
                    RECIP(cd.ap[:, 0:nq], cd.ap[:, 0:nq], [cd.b], [cd.b])
                    TT("dve", os_.ap[:, 0:nq], cn.ap[:, 0:nq], cd.ap[:, 0:nq], ALU.mult, [cn.b, cd.b], [os_.b])
                DMA("sp", oT[hp, :, q0:q0 + nq], os_.ap[:, 0:nq], [os_.b], [b_oTa[tq]])

            for b in range(2):
                tl = [4 * b + i for i in range(4)]
                DMA("sp", kt3[0:KP, :, 0:256], KTd[:, :, 4096 + 256 * b:4096 + 256 * (b + 1)].rearrange("k p t -> p k t"),
                    [b_KT[8]], [kt.b])
                DMA("sp", kt3[0:KP, :, 256:2304], KTd[:, :, 2048 * b:2048 * (b + 1)].rearrange("k p t -> p k t"),
                    [b_KT[i] for i in tl], [kt.b])
                DMA("sp", va3[:, 0:2, :], VAd[32 + 2 * b:34 + 2 * b].rearrange("s p f -> p s f"), [b_VA[8]], [va.b])
                DMA("sp", va3[:, 2:18, :], VAd[16 * b:16 * (b + 1)].rearrange("s p f -> p s f"), [b_VA[i] for i in tl], [va.b])
                jobs = []
                for qi in range(4):
                    for hp in range(4):
                        jobs.append((hp, 2048 * b + 512 * qi, 512, 18, 4 * b + qi))
                if layer == 0:
                    for hp in range(4):
                        jobs.append((hp, 4096 + 256 * b, 256, 2, 8))
                qcur = load_q(jobs[0])
                for ji, job in enumerate(jobs):
                    qnext = load_q(jobs[ji + 1]) if ji + 1 < len(jobs) else None
                    run_job(job, qcur)
                    qcur = qnext
                    if filler is not None and ji % 4 == 3 and ji < 16:
                        filler[1](fst, b, ji // 4)
                        if PREFETCH_D and layer == 0 and b == 0 and ji == 3:
                            prefetch_D(0, which=("w1",), after=[b_oTa[0]])
            if filler is not None:
                filler[2](fst)
                PS.items = psb[0:7]
            P.barrier()

        W2_OFF, W1_OFF = 72 * 1024, 136 * 1024

        def d_weights():
            wa = T(arena_t[:, W1_OFF // 4:(W1_OFF + 65536) // 4].bitcast(BF16), "w1")
            wb = T(arena_t[:, W2_OFF // 4:(W2_OFF + 65536) // 4].bitcast(BF16), "w2")
            return wa, wb

        dw = {}

        def prefetch_D(L, which=("w1", "w2"), nobar=False, after=()):
            if L not in dw:
                wa, wb = d_weights()
                dw[L] = [wa, wb, set()]
            wa, wb, loaded = dw[L]
            wa3, wb3 = v3(wa.ap, 8), v3(wb.ap, 32)
            if "w1" in which and "w1" not in loaded:
                loaded.add("w1")
                for i in range(8):
                    DMA("pool", wa3[:, i:i + 1], w1[L][:, i:i + 1], list(after), [wa.b], nobar=nobar)
            if "w2" in which and "w2" not in loaded:
                loaded.add("w2")
                for i in range(8):
                    DMA("pool", wb3[:, 4 * i:4 * i + 4], w2[L][:, 4 * i:4 * i + 4], list(after), [wb.b], nobar=nobar)

        def c_alloc(L, limit=None):
            w = A.bf16(8 * 1024, "wout")
            cast_load(v3(w.ap, 8), w_out[L], w.b, pieces=4)
            xts = Rot([A.f32(4096, "x%d" % i) for i in range(2)])
            ots = Rot([A.bf16(4096, "o%d" % i) for i in range(2)])
            if limit is not None:
                assert A.off <= limit, A.off
            return (w, xts, ots)

        def c_tile(stt, L, xsrc, t):
            w, xts, ots = stt
            w3 = v3(w.ap, 8)
            row = row_of(t)
            xt, ot = xts.next(), ots.next()
            load_x(xt, xsrc, t)
            DMA("sp", v3(ot.ap, 8), oT[:, :, cols(t)].rearrange("c p t -> p c t"), [b_oTa[t], b_oTb[t]], [ot.b])
            for n in range(8):
                p = PS.next()
                for c in range(8):
                    MM(p.ap, w3[:, c, n * 128:(n + 1) * 128], ot.ap[:, c * 512:(c + 1) * 512], c == 0, c == 7, [w.b, ot.b], [p.b])
                STT(xt.ap[:, n * 512:(n + 1) * 512], p.ap, G1(n, row), xt.ap[:, n * 512:(n + 1) * 512], ALU.mult, ALU.add,
                    [p.b, xt.b, modT.b], [xt.b])
            store_x(xt, xT, t, b_xT[t])

        def phase_C(L, xsrc, tiles):
            A.off = BASE
            stt = c_alloc(L, W2_OFF)
            for ti, t in enumerate(tiles):
                c_tile(stt, L, xsrc, t)
                if PREFETCH_D and ti == 0:
                    prefetch_D(L, after=[b_xT[t]])
            P.barrier()

        def filler_C(L, xsrc):
            return (lambda: c_alloc(L),
                    lambda stt, b, qi: c_tile(stt, L, xsrc, 4 * b + qi),
                    lambda stt: c_tile(stt, L, xsrc, 8))

        def phase_D(L, tiles, final):
            A.off = BASE
            prefetch_D(L)
            wa, wb = dw[L][0], dw[L][1]
            wa3, wb3 = v3(wa.ap, 8), v3(wb.ap, 32)
            xts = Rot([A.f32(4096, "x%d" % i) for i in range(2)])
            rss = Rot([A.f32(512, "rs%d" % i) for i in range(2)])
            ht = A.bf16(4096, "h2")
            sq = T(ht.ap, "sq_alias")
            sq.b = ht.b
            hid0 = A.off
            hid = [A.bf16(512, "hid%d" % i) for i in range(16)]
            sqf = T(arena_t[:, hid0 // 4: hid0 // 4 + 2048].bitcast(BF16), "sqf")
            sqf_bufs = [h.b for h in hid[0:8]]
            tmps = Rot([A.f32(512, "tm%d" % i) for i in range(3)])
            assert A.off <= W2_OFF, A.off
            for t in tiles:
                row = row_of(t)
                xt, rs = xts.next(), rss.next()
                load_x(xt, xT, t)
                rms_stats(xt, sq, rs)
                modulate(xt, rs, A2v, SH2, row, ht, tmps)
                for half in range(2):
                    for jj in range(16):
                        j = half * 16 + jj
                        p = PS.next()
                        for c in range(8):
                            MM(p.ap, wa3[:, c, j * 128:(j + 1) * 128], ht.ap[:, c * 512:(c + 1) * 512], c == 0, c == 7,
                               [wa.b, ht.b], [p.b])
                        tm = tmps.next()
                        ACT(tm.ap, p.ap, AF.Relu, [p.b], [tm.b])
                        TT("pool", hid[jj].ap, tm.ap, tm.ap, ALU.mult, [tm.b], [hid[jj].b])
                    for n in range(8):
                        p = PS.next()
                        for jj in range(16):
                            j = half * 16 + jj
                            MM(p.ap, wb3[:, j, n * 128:(n + 1) * 128], hid[jj].ap, jj == 0, jj == 15, [wb.b, hid[jj].b], [p.b])
                        STT(xt.ap[:, n * 512:(n + 1) * 512], p.ap, G2(n, row), xt.ap[:, n * 512:(n + 1) * 512], ALU.mult, ALU.add,
                            [p.b, xt.b, modT.b], [xt.b])
                if not final:
                    store_x(xt, xT, t, b_xT[t])
                else:
                    rms_stats(xt, sqf, rs, sqbufs=sqf_bufs)
                    for c in range(8):
                        STT(xt.ap[:, c * 512:(c + 1) * 512], xt.ap[:, c * 512:(c + 1) * 512], gfin.ap[:, c:c + 1], rs.ap,
                            ALU.mult, ALU.mult, [xt.b, rs.b, bconst], [xt.b])
                    store_x(xt, outT, t, b_out)
            P.barrier()

        def phase_A1():
            A.off = BASE
            w = A.bf16(8 * 1440, "w_in1")
            w3 = v3(w.ap, 8)
            cast_load(w3, od_w_in, w.b, pieces=8)
            wuq = A.bf16(2 * 768, "wuq")
            wuq3 = v3(wuq.ap, 2)
            cast_load(wuq.ap, od_wuq.rearrange("p c n -> p (c n)"), wuq.b)
            wukv = A.bf16(1024, "wukv")
            cast_load(wukv.ap, od_wukv, wukv.b)
            gq = A.f32(256, "gq1")
            gkv = A.f32(128, "gkv1")
            cs = A.f32(2 * 17 * 32, "cs1")
            DMA("sp", gq.ap, od_gq, [], [gq.b])
            DMA("sp", gkv.ap, od_gkv, [], [gkv.b])
            DMA("sp", cs.ap, od_cs.rearrange("p a b d -> p (a b d)"), [], [cs.b])
            cs4 = cs.ap.rearrange("p (a b d) -> p a b d", a=2, b=17)

            xt = A.f32(4096, "x")
            sq = A.bf16(4096, "sq")
            rs = A.f32(512, "rs")
            hts = Rot([A.bf16(4096, "h%d" % i) for i in range(2)])
            tmps = Rot([A.f32(512, "tm%d" % i) for i in range(3)])
            QTst = Rot([A.bf16(8 * 512, "QTst%d" % i) for i in range(2)])
            KTst = Rot([A.bf16(8 * 512, "KTst%d" % i) for i in range(2)])
            VAl = [A.bf16(4 * 1024, "VAst%d" % i) for i in range(2)]
            for v in VAl:
                MEMSET(v.ap, 1.0, [v.b])
            VAst = Rot(VAl)
            Yst = Rot([A.bf16(4 * 512, "Yst%d" % i) for i in range(2)])
            sqp = mk(384, "sqp")
            ss2 = mk(2, "ss2")
            cqn = mk(256, "cqn", f32=False)
            ckvn = mk(128, "ckvn", f32=False)
            cqT = mk(256, "cqT", f32=False)
            ckT = mk(128, "ckT", f32=False)
            kt1 = mk(32, "kt1")
            kt2 = mk(32, "kt2")
            qa = mk(768, "qa", f32=False)
            ka = mk(768, "ka", f32=False)
            qt1 = mk(256, "qt1")
            qt2 = mk(256, "qt2")
            sg = mk(512, "sg")

            order = [8] + list(range(8))
            load_x(xt, xT, order[0])
            for ti, t in enumerate(order):
                lat = t < 8
                row = row_of(t)
                rms_stats(xt, sq, rs)
                ht = hts.next()
                modulate(xt, rs, A1v, SH1, row, ht, tmps)
                if ti + 1 < len(order):
                    load_x(xt, xT, order[ti + 1])
                qst, kst, vst, yst = QTst.next(), KTst.next(), VAst.next(), Yst.next()
                for s in range(4):
                    blk = 16 if t == 8 else (t % 4) * 4 + s
                    sl = slice(s * 128, (s + 1) * 128)
                    C3 = cs4[:, 0, blk, :]
                    S3 = cs4[:, 1, blk, :]
                    pP = PS.next()
                    for c in range(8):
                        MM(pP.ap[:, 0:416], ht.ap[:, c * 512 + s * 128:c * 512 + s * 128 + 128], w3[:, c, 0:416], c == 0, c == 7,
                           [ht.b, w.b], [pP.b])
                    sqp_, ss_ = sqp.next(), ss2.next()
                    ACT(sqp_.ap, pP.ap[:, 0:384], AF.Square, [pP.b], [sqp_.b])
                    RED(ss_.ap[:, 0:1], sqp_.ap[:, 0:256], [sqp_.b], [ss_.b])
                    RED(ss_.ap[:, 1:2], sqp_.ap[:, 256:384], [sqp_.b], [ss_.b])
                    ACT(ss_.ap[:, 0:1], ss_.ap[:, 0:1], AF.Sqrt, [ss_.b, bconst], [ss_.b], bias=eps_t.ap[:, 0:1], scale=1.0 / 256)
                    ACT(ss_.ap[:, 1:2], ss_.ap[:, 1:2], AF.Sqrt, [ss_.b, bconst], [ss_.b], bias=eps_t.ap[:, 0:1], scale=1.0 / 128)
                    RECIP(ss_.ap, ss_.ap, [ss_.b], [ss_.b])
                    ckvn_, ckT_ = ckvn.next(), ckT.next()
                    STT(ckvn_.ap, pP.ap[:, 256:384], ss_.ap[:, 1:2], gkv.ap, ALU.mult, ALU.mult, [pP.b, ss_.b, gkv.b], [ckvn_.b])
                    pT = PS.next()
                    pTb = ps_bf(pT)
                    TR(pTb[:, 0:128], ckvn_.ap, [ckvn_.b], [pT.b])
                    cqT_ = None
                    if lat:
                        cqn_, cqT_ = cqn.next(), cqT.next()
                        STT(cqn_.ap, pP.ap[:, 0:256], ss_.ap[:, 0:1], gq.ap, ALU.mult, ALU.mult, [pP.b, ss_.b, gq.b], [cqn_.b])
                        for k in range(2):
                            TR(pTb[:, 128 + k * 128:256 + k * 128], cqn_.ap[:, k * 128:(k + 1) * 128], [cqn_.b], [pT.b])
                        ACT(cqT_.ap, pTb[:, 128:384], AF.Copy, [pT.b], [cqT_.b])
                    ACT(ckT_.ap, pTb[:, 0:128], AF.Copy, [pT.b], [ckT_.b])
                    ka_ = ka.next()
                    ka3 = ka_.ap.rearrange("p (h d) -> p h d", h=8)
                    pK = PS.next()
                    MM(pK.ap, ckT_.ap, wukv.ap[:, 0:512], True, True, [ckT_.b, wukv.b], [pK.b])
                    ACT(ka3[:, :, 0:64], pK.ap.rearrange("p (h d) -> p h d", h=8), AF.Copy, [pK.b], [ka_.b])
                    kt1_, kt2_ = kt1.next(), kt2.next()
                    rope(pP, pP.ap[:, 384:416], 1, 32, C3, S3, [cs.b], kt1_, kt2_, kt1_.ap.rearrange("p (h d) -> p h d", h=1), kt1_)
                    COPY("dve", ka3[:, :, 64:96], kt1_.ap.unsqueeze(1).to_broadcast([128, 8, 32]), [kt1_.b], [ka_.b])
                    pTK = PS.next()
                    pTKb = ps_bf(pTK)
                    for h in range(8):
                        TR(pTKb[0:96, h * 128:(h + 1) * 128], ka3[:, h, :], [ka_.b], [pTK.b])
                    COPY("dve", v3(kst.ap, 8)[0:96, :, sl], v3(pTKb[0:96, :], 8), [pTK.b], [kst.b])
                    pV = PS.next()
                    MM(pV.ap, ckT_.ap, wukv.ap[:, 512:1024], True, True, [ckT_.b, wukv.b], [pV.b])
                    v5 = vst.ap.rearrange("p (s h a d) -> p s h a d", s=4, h=4, a=2)
                    pV4 = pV.ap.rearrange("p (h a d) -> p h a d", h=4, a=2)
                    ACT(v5[:, s, :, 0, 0:64], pV4[:, :, 0, :], AF.Copy, [pV.b], [vst.b])
                    ACT(v5[:, s, :, 1, 64:128], pV4[:, :, 1, :], AF.Copy, [pV.b], [vst.b])
                    if lat:
                        qa_ = qa.next()
                        qa3 = qa_.ap.rearrange("p (h d) -> p h d", h=8)
                        pQ1, pQ2 = PS.next(), PS.next()
                        for k in range(2):
                            MM(pQ1.ap, cqT_.ap[:, k * 128:(k + 1) * 128], wuq3[:, k, 0:512], k == 0, k == 1, [cqT_.b, wuq.b], [pQ1.b])
                        for k in range(2):
                            MM(pQ2.ap[:, 0:256], cqT_.ap[:, k * 128:(k + 1) * 128], wuq3[:, k, 512:768], k == 0, k == 1,
                               [cqT_.b, wuq.b], [pQ2.b])
                        ACT(qa3[:, :, 0:64], pQ1.ap.rearrange("p (h d) -> p h d", h=8), AF.Copy, [pQ1.b], [qa_.b])
                        rope(pQ2, pQ2.ap[:, 0:256], 8, 32, C3, S3, [cs.b], qt1.next(), qt2.next(), qa3[:, :, 64:96], qa_)
                        pTQ = PS.next()
                        pTQb = ps_bf(pTQ)
                        for h in range(8):
                            TR(pTQb[0:96, h * 128:(h + 1) * 128], qa3[:, h, :], [qa_.b], [pTQ.b])
                        COPY("dve", v3(qst.ap, 8)[0:96, :, sl], v3(pTQb[0:96, :], 8), [pTQ.b], [qst.b])
                if lat:
                    for j in range(4):
                        pa, pg = PS.next(), PS.next()
                        for (pp, c0) in ((pa, 416 + j * 128), (pg, 416 + 512 + j * 128)):
                            for c in range(8):
                                MM(pp.ap, w3[:, c, c0:c0 + 128], ht.ap[:, c * 512:(c + 1) * 512], c == 0, c == 7, [w.b, ht.b], [pp.b])
                        sg_ = sg.next()
                        ACT(sg_.ap, pg.ap, AF.Sigmoid, [pg.b], [sg_.b])
                        TT("dve", yst.ap[:, j * 512:(j + 1) * 512], pa.ap, sg_.ap, ALU.mult, [pa.b, sg_.b], [yst.b])
                    DMA("sp", QT1[:, :, cols(t)].rearrange("h p t -> p h t"), v3(qst.ap, 8)[0:96], [qst.b], [b_QT[t]])
                    DMA("sp", YT[:, :, cols(t)].rearrange("c p t -> p c t"), v3(yst.ap, 4), [yst.b], [b_YT[t]])
                DMA("sp", KT1[:, :, cols(t)].rearrange("h p t -> p h t"), v3(kst.ap, 8)[0:96], [kst.b], [b_KT[t]])
                DMA("sp", VA1[t * 4:(t + 1) * 4].rearrange("s p f -> p s f"), v3(vst.ap, 4), [vst.b], [b_VA[t]])
            P.barrier()

        def e_alloc():
            cw = A.f32(4 * 31, "cw")
            vec = A.f32(12, "vec")
            DMA("sp", cw.ap, od_cw.rearrange("p j k -> p (j k)"), [], [cw.b])
            DMA("sp", vec.ap, od_vec.rearrange("p a j -> p (a j)"), [], [vec.b])
            D = A.bf16(4 * 31 * 128, "D")
            Db = [Buf() for _ in range(124)]
            for i in range(124):
                TS("dve" if i % 3 != 2 else "pool", D.ap[:, i * 128:(i + 1) * 128], ident_f.ap, cw.ap[:, i:i + 1], ALU.mult,
                   [cw.b, bconst], [Db[i]])
            ypad = A.bf16(4 * 2080, "ypad")
            MEMSET(ypad.ap, 0.0, [ypad.b])
            yc = A.f32(4 * 512, "yc")
            ysq = A.f32(4 * 512, "ysq")
            mu = A.f32(512, "mu")
            msq = A.f32(512, "msq")
            rsd = A.f32(512, "rsd")
            tms = Rot([A.f32(512, "tl%d" % i) for i in range(2)])
            ost = Rot([A.bf16(4 * 512, "ost%d" % i) for i in range(2)])
            return (vec, D, Db, ypad, yc, ysq, mu, msq, rsd, tms, ost)

        def e_tile(stt, b, qi):
            (vec, D, Db, ypad, yc, ysq, mu, msq, rsd, tms, ost) = stt
            yp3 = v3(ypad.ap, 4)
            if qi == 0:
                DMA("sp", yp3[:, :, 15:15 + 2048], YT[:, :, 2048 * b:2048 * (b + 1)].rearrange("c p t -> p c t"),
                    [b_YT[4 * b + i] for i in range(4)], [ypad.b])
            t = 4 * b + qi
            t0 = qi * 512
            for j in range(4):
                p = PS.next()
                for k in range(31):
                    MM(p.ap, D.ap[:, (j * 31 + k) * 128:(j * 31 + k + 1) * 128], yp3[:, j, t0 + k:t0 + k + 512],
                       k == 0, k == 30, [Db[j * 31 + k], ypad.b], [p.b])
                TS("dve", yc.ap[:, j * 512:(j + 1) * 512], p.ap, vec.ap[:, j:j + 1], ALU.add, [p.b, vec.b], [yc.b])
            TT("pool", ysq.ap, yc.ap, yc.ap, ALU.mult, [yc.b], [ysq.b])
            pm, pq = PS.next(), PS.next()
            for j in range(4):
                MM(pm.ap, ones_f.ap, yc.ap[:, j * 512:(j + 1) * 512], j == 0, j == 3, [yc.b, bconst], [pm.b])
            TS("dve", mu.ap, pm.ap, 1.0 / 512, ALU.mult, [pm.b], [mu.b])
            for j in range(4):
                MM(pq.ap, ones_f.ap, ysq.ap[:, j * 512:(j + 1) * 512], j == 0, j == 3, [ysq.b, bconst], [pq.b])
            TT("pool", msq.ap, mu.ap, mu.ap, ALU.mult, [mu.b], [msq.b])
            STT(rsd.ap, pq.ap, 1.0 / 512, msq.ap, ALU.mult, ALU.subtract, [pq.b, msq.b], [rsd.b])
            ACT(rsd.ap, rsd.ap, AF.Sqrt, [rsd.b, bconst], [rsd.b], bias=eps_t.ap[:, 0:1], scale=1.0)
            RECIP(rsd.ap, rsd.ap, [rsd.b], [rsd.b])
            os_ = ost.next()
            for j in range(4):
                tm = tms.next()
                TT("dve", tm.ap, yc.ap[:, j * 512:(j + 1) * 512], mu.ap, ALU.subtract, [yc.b, mu.b], [tm.b])
                TT("pool", tm.ap, tm.ap, rsd.ap, ALU.mult, [tm.b, rsd.b], [tm.b])
                ACT(os_.ap[:, j * 512:(j + 1) * 512], tm.ap, AF.Silu, [tm.b, vec.b], [os_.b],
                    bias=vec.ap[:, 8 + j:9 + j], scale=vec.ap[:, 4 + j:5 + j])
            DMA("sp", oT[4:8, :, cols(t)].rearrange("c p t -> p c t"), v3(os_.ap, 4), [os_.b], [b_oTb[t]])

        def phase_E1():
            A.off = BASE
            stt = e_alloc()
            for b in range(2):
                for qi in range(4):
                    e_tile(stt, b, qi)
            P.barrier()

        def filler_E1():
            return (e_alloc, e_tile, lambda stt: None)

        P.barrier()
        if upto >= 0:
            phase_M([0, 1])
            set_layer(0)
        if upto >= 1:
            phase_A0(xT_in)
        if upto >= 2:
            if MERGE_FILL:
                phase_B(0, filler_C(0, xT_in))
            else:
                phase_B(0)
        if upto >= 3 and not MERGE_FILL:
            phase_C(0, xT_in, list(range(9)))
        if upto >= 4:
            phase_D(0, list(range(9)), final=False)
        if upto >= 5:
            set_layer(1)
            phase_A1()
        if upto >= 6:
            if MERGE_FILL:
                phase_B(1, filler_E1())
            else:
                phase_B(1)
        if upto >= 7 and not MERGE_FILL:
            phase_E1()
        if upto >= 8:
            phase_C(1, xT, list(range(8)))
        if upto >= 9:
            phase_D(1, list(range(8)), final=True)
        dbg = []
        if debug_dump is not None:
            dbg = debug_dump(nc, P, locals())
        P.add("sp", lambda e: e.nop(), [b_out] + list(dbg))
        stats = P.emit(nc, st)
    return nc, stats


def _rope_tables(d_rot):
    GRID_W = 64
    pos = np.arange(2048)
    row = (pos // GRID_W).astype(np.float64)
    col = (pos % GRID_W).astype(np.float64)
    d_axis = d_rot // 2
    inv = 10000.0 ** (-np.arange(0, d_axis, 2, dtype=np.float64) / d_axis)
    ang = np.concatenate([row[:, None] * inv, col[:, None] * inv], axis=-1).astype(np.float32)
    ang = ang.astype(np.float64)
    cos, sin = np.cos(ang), np.sin(ang)
    cos2 = np.repeat(cos, 2, axis=1)
    sins = np.stack([-sin, sin], axis=-1).reshape(2048, d_rot)
    out = np.zeros((128, 2, 17, d_rot), np.float32)
    out[:, 0, :16] = cos2.reshape(16, 128, d_rot).transpose(1, 0, 2)
    out[:, 1, :16] = sins.reshape(16, 128, d_rot).transpose(1, 0, 2)
    out[:, 0, 16] = 1.0
    out[:, 1, 16] = 0.0
    return out


def _pairswap(g):
    return g.reshape(-1, 2)[:, ::-1].reshape(-1)


def _rep(v):
    return np.ascontiguousarray(np.broadcast_to(np.asarray(v, np.float32).reshape(1, -1), (128, v.size)))


def _kc(w):
    K, N = w.shape
    return np.ascontiguousarray(w.reshape(K // 128, 128, N).transpose(1, 0, 2))


def _vecT(v):
    return np.ascontiguousarray(v.reshape(-1, 128).T)


def prep_shared(inp):
    f = lambda a: np.asarray(a, np.float32)
    sh = {}
    ada_w = f(inp["ada_w"])
    sh["ada_w"] = np.ascontiguousarray(ada_w.reshape(2, 8, 128, 6, 1024).transpose(0, 3, 2, 1, 4))
    sh["ada_b"] = np.ascontiguousarray(f(inp["ada_b"]).reshape(2, 48, 128).transpose(0, 2, 1))
    sh["n1g"] = np.ascontiguousarray(f(inp["norm1_g"]).reshape(2, 8, 128).transpose(0, 2, 1))
    sh["n2g"] = np.ascontiguousarray(f(inp["norm2_g"]).reshape(2, 8, 128).transpose(0, 2, 1))
    sh["fing"] = _vecT(f(inp["final_g"]))
    sh["w_out"] = np.stack([_kc(f(inp["w_out"])[i]) for i in range(2)])
    sh["w1"] = np.stack([_kc(f(inp["mlp_w1"])[i]) for i in range(2)])
    sh["w2"] = np.stack([_kc(f(inp["mlp_w2"])[i]) for i in range(2)])
    sh["ev_w_in"] = _kc(f(inp["ev_w_in"])[0])
    gq = f(inp["ev_q_norm_g"])[0]
    gk = f(inp["ev_k_norm_g"])[0]
    sh["ev_gq"] = np.ascontiguousarray(np.stack([_rep(gq), _rep(_pairswap(gq))], axis=1))
    sh["ev_gk"] = np.ascontiguousarray(np.stack([_rep(gk), _rep(_pairswap(gk))], axis=1))
    sh["ev_cs"] = _rope_tables(64)
    sh["ev_sgug"] = _rep(f(inp["ev_sgu_norm_g"])[0].reshape(-1))
    sh["ev_wsT"] = np.ascontiguousarray(f(inp["ev_sgu_w"])[0].transpose(2, 0, 1))
    sh["ev_bsT"] = np.ascontiguousarray(f(inp["ev_sgu_b"])[0].T)
    sh["od_w_in"] = _kc(f(inp["od_w_in"])[0])
    sh["od_gq"] = _rep(f(inp["od_q_norm_g"])[0])
    sh["od_gkv"] = _rep(f(inp["od_kv_norm_g"])[0])
    wuq = f(inp["od_w_uq"])[0].reshape(256, 8, 96)
    wuq = np.concatenate([wuq[:, :, :64].reshape(256, 512), wuq[:, :, 64:].reshape(256, 256)], axis=1)
    sh["od_wuq"] = _kc(wuq)
    wukv = f(inp["od_w_ukv"])[0].reshape(128, 8, 128)
    sh["od_wukv"] = np.ascontiguousarray(np.concatenate([wukv[:, :, :64].reshape(128, 512), wukv[:, :, 64:].reshape(128, 512)], axis=1))
    sh["od_cs"] = _rope_tables(32)
    sh["od_cw"] = np.ascontiguousarray(f(inp["od_conv_w"])[0].T.reshape(4, 128, 31).transpose(1, 0, 2))
    sh["od_vec"] = np.ascontiguousarray(np.stack([_vecT(f(inp["od_conv_b"])[0]), _vecT(f(inp["od_ln_g"])[0]),
                                                  _vecT(f(inp["od_ln_b"])[0])], axis=1))
    return sh


def prep_core(inp, k):
    x = np.asarray(inp["x"], np.float32)
    ctx = np.asarray(inp["ctx"], np.float32)
    c = np.asarray(inp["c"], np.float32)
    cc = np.asarray(inp["c_ctx"], np.float32)
    b0, b1 = 2 * k, 2 * k + 1
    tok = np.concatenate([x[b0], x[b1], ctx[b0], ctx[b1]], axis=0)
    xT = np.ascontiguousarray(tok.T).reshape(8, 128, NT)
    rows = np.stack([c[b0], c[b1], cc], axis=0)
    cT = np.ascontiguousarray(rows.reshape(3, 8, 128).transpose(2, 1, 0))
    return {"xT_in": xT, "cT": cT}


_CACHE = {}


def kernel(**inputs):
    if "nc" not in _CACHE:
        _CACHE["nc"] = build()[0]
    nc = _CACHE["nc"]
    sh = prep_shared(inputs)
    in_maps = []
    for k in range(8):
        m = dict(sh)
        m.update(prep_core(inputs, k))
        in_maps.append(m)
    res = run_bass_kernel_spmd(nc, in_maps, core_ids=list(range(8)))
    out = np.empty((16, 2048, 1024), np.float32)
    for k in range(8):
        o = np.asarray(res.results[k]["outT"]).reshape(1024, NLAT)
        out[2 * k] = o[:, 0:2048].T
        out[2 * k + 1] = o[:, 2048:4096].T
    return out
```

```python
import numpy as np
from contextlib import ExitStack
import concourse.bass as bass
import concourse.mybir as mybir
from concourse.bass_utils import run_bass_kernel_spmd

F32 = mybir.dt.float32
BF16 = mybir.dt.bfloat16
AF = mybir.ActivationFunctionType
ALU = mybir.AluOpType
AX = mybir.AxisListType

EPS = 1e-6
PREFETCH_D = True
MERGE_FILL = True
NT = 4608
NLAT = 4096
NTILE = 9


class Buf:
    __slots__ = ("name", "w", "r", "excl")

    def __init__(self, name=""):
        self.name = name
        self.w = None
        self.r = []
        self.excl = False


class Op:
    __slots__ = ("eng", "fn", "dma", "deps", "sig", "sigidx", "chan", "chanval", "cost", "alldeps", "pos", "nobar", "tag", "tbl")

    def __init__(self, eng, fn, dma, deps, cost=300.0):
        self.eng = eng
        self.fn = fn
        self.dma = dma
        self.deps = deps
        self.sig = False
        self.sigidx = 0
        self.chan = None
        self.chanval = 0
        self.cost = cost
        self.alldeps = None
        self.pos = 0
        self.nobar = False
        self.tbl = None


ENGS = ["pe", "act", "dve", "pool", "sp"]
NCHAN = {"sp": 20, "pool": 10, "act": 4}
SCHED_WINDOW = 48
HOP_NS = 150.0
ACT_TABLE_NS = 1300.0


class Prog:
    def __init__(self):
        self.ops = []
        self.chan_rr = {q: 0 for q in NCHAN}
        self.chan_last = {}
        self.chan_cnt = {}
        self.last_eng = {}
        self.bar = {}

    def add(self, eng, fn, reads=(), writes=(), dma=False, cost=300.0, nobar=False):
        i = len(self.ops)
        deps = set()
        for b in reads:
            if b.w is not None:
                deps.add(b.w)
            if b.excl:
                for j in b.r:
                    if self.ops[j].eng != eng:
                        deps.add(j)
        for b in writes:
            if b.w is not None:
                deps.add(b.w)
            deps.update(b.r)
        for b in reads:
            b.r.append(i)
        for b in writes:
            b.w = i
            b.r = []
        deps.discard(i)
        op = Op(eng, fn, dma, deps, cost)
        op.nobar = nobar
        op.tag = getattr(self, "curtag", "")
        if dma:
            c = self.chan_rr[eng]
            self.chan_rr[eng] = (c + 1) % NCHAN[eng]
            key = (eng, c)
            if key in self.chan_last:
                op.deps.add(self.chan_last[key])
            self.chan_last[key] = i
            self.chan_cnt[key] = self.chan_cnt.get(key, 0) + 1
            op.chan = key
            op.chanval = 16 * self.chan_cnt[key]
        op.alldeps = set(op.deps)
        if eng in self.bar:
            op.alldeps.add(self.bar[eng])
        self.last_eng[eng] = i
        self.ops.append(op)
        return i

    def pe(self, fn, r=(), w=(), cost=300.0):
        return self.add("pe", fn, r, w, cost=cost)

    def act(self, fn, r=(), w=(), cost=300.0):
        return self.add("act", fn, r, w, cost=cost)

    def dve(self, fn, r=(), w=(), cost=300.0):
        return self.add("dve", fn, r, w, cost=cost)

    def pool(self, fn, r=(), w=(), cost=300.0):
        return self.add("pool", fn, r, w, cost=cost)

    def dma(self, q, fn, r=(), w=(), cost=4000.0, nobar=False):
        return self.add(q, fn, r, w, dma=True, cost=cost, nobar=nobar)

    def barrier(self):
        deps = set(self.last_eng.values())
        for key, i in self.chan_last.items():
            j = i
            if not self.ops[j].nobar:
                deps.add(j)
        for e in ENGS:
            i = len(self.ops)
            op = Op(e, lambda eng: eng.nop(), False, set(deps), 30.0)
            op.alldeps = set(deps)
            if e in self.bar:
                op.alldeps.add(self.bar[e])
            self.ops.append(op)
            self.last_eng[e] = i
            self.bar[e] = i

    def schedule(self):
        import bisect
        ops = self.ops
        n = len(ops)
        users = [[] for _ in range(n)]
        indeg = [0] * n
        for i, op in enumerate(ops):
            indeg[i] = len(op.alldeps)
            for d in op.alldeps:
                users[d].append(i)
        level = [0.0] * n
        for i in range(n - 1, -1, -1):
            m = 0.0
            for u in users[i]:
                if level[u] > m:
                    m = level[u]
            level[i] = ops[i].cost + m
        avail = {e: [] for e in ENGS}
        ready_t = [0.0] * n
        finish = [0.0] * n
        for i, op in enumerate(ops):
            if indeg[i] == 0:
                avail[op.eng].append(i)
        eng_free = {e: 0.0 for e in ENGS}
        order = {e: [] for e in ENGS}
        done = 0
        cur_tbl = None
        while done < n:
            best = None
            for e in ENGS:
                av = avail[e]
                if not av:
                    continue
                ef = eng_free[e]
                cb = None
                for i in av[:SCHED_WINDOW]:
                    st = ready_t[i] if ready_t[i] > ef else ef
                    if e == "act" and ops[i].tbl is not None and ops[i].tbl != cur_tbl:
                        st += ACT_TABLE_NS
                    key = (st, -level[i], i)
                    if cb is None or key < cb:
                        cb = key
                if best is None or cb < best[0]:
                    best = (cb, e)
            (st, _lv, i), e = best
            op = ops[i]
            avail[e].remove(i)
            if e == "act" and op.tbl is not None:
                cur_tbl = op.tbl
            if op.dma:
                eng_free[e] = st + 60.0
            else:
                eng_free[e] = st + op.cost
            finish[i] = st + op.cost
            op.pos = len(order[e])
            order[e].append(i)
            done += 1
            for u in users[i]:
                indeg[u] -= 1
                t = finish[i] + (HOP_NS if ops[u].eng != e or op.dma else 0.0)
                if t > ready_t[u]:
                    ready_t[u] = t
                if indeg[u] == 0:
                    bisect.insort(avail[ops[u].eng], u)
        self.est_ns = max(finish) if n else 0.0
        self.finish = finish
        self.order = order
        return order

    def emit(self, nc, stack):
        ops = self.ops
        order = self.schedule()
        seen = {e: {} for e in ENGS}
        for e in ENGS:
            sd = seen[e]
            for i in order[e]:
                op = ops[i]
                red = {}
                for d in op.deps:
                    p = ops[d]
                    if p.dma:
                        key = ("c", p.chan)
                        val = p.chanval
                    else:
                        if p.eng == e and e == "pe":
                            continue
                        key = ("e", p.eng)
                        val = p.pos
                    if key not in red or val > red[key][0]:
                        red[key] = (val, d)
                keep = []
                for key, (val, d) in red.items():
                    if sd.get(key, -1) >= val:
                        continue
                    sd[key] = val
                    keep.append(d)
                op.deps = keep
        for op in ops:
            for d in op.deps:
                p = ops[d]
                if not p.dma:
                    p.sig = True
        cnt = {e: 0 for e in ENGS}
        for e in ENGS:
            for i in order[e]:
                op = ops[i]
                if op.sig:
                    cnt[e] += 1
                    op.sigidx = cnt[e]
        esem = {e: stack.enter_context(nc.semaphore("s_" + e)) for e in ENGS}
        csem = {}
        for key in self.chan_cnt:
            csem[key] = stack.enter_context(nc.semaphore("c_%s%d" % key))
        block = stack.enter_context(nc.Block())
        handles = {"pe": block.tensor, "act": block.scalar, "dve": block.vector,
                   "pool": block.gpsimd, "sp": block.sync}
        stats = {"sig": dict(cnt), "est_us": self.est_ns / 1e3}
        for e in ENGS:
            my = [ops[i] for i in order[e]]
            stats[e] = len(my)
            if not my:
                continue

            def body(eng, my=my, e=e):
                nw = 0
                for op in my:
                    for d in op.deps:
                        p = ops[d]
                        if p.dma:
                            eng.wait_ge(csem[p.chan], p.chanval)
                        else:
                            eng.wait_ge(esem[p.eng], p.sigidx)
                        nw += 1
                    ins = op.fn(eng)
                    if op.dma:
                        ins.then_inc(csem[op.chan], 16)
                    elif op.sig:
                        ins.then_inc(esem[e], 1)
                stats[e + "_waits"] = nw

            handles[e](body)
        return stats


class T:
    __slots__ = ("ap", "b")

    def __init__(self, ap, name=""):
        self.ap = ap
        self.b = Buf(name)


class Arena:
    def __init__(self, t, cap):
        self.t = t
        self.cap = cap
        self.off = 0

    def f32(self, n, name=""):
        assert self.off % 4 == 0
        a = self.t[:, self.off // 4: self.off // 4 + n]
        self.off += n * 4
        assert self.off <= self.cap, ("SBUF overflow", self.off, name)
        return T(a, name)

    def bf16(self, n, name=""):
        n2 = (n + 1) // 2
        a = self.t[:, self.off // 4: self.off // 4 + n2].bitcast(BF16)[:, 0:n]
        self.off += n2 * 4
        assert self.off <= self.cap, ("SBUF overflow", self.off, name)
        return T(a, name)


class Rot:
    def __init__(self, items):
        self.items = items
        self.i = 0

    def next(self):
        x = self.items[self.i % len(self.items)]
        self.i += 1
        return x


ACT_TBL = {AF.Sqrt: "sqrt", AF.Gelu_apprx_tanh: "gelu", AF.Exp: "exp", AF.Sigmoid: "sigmoid", AF.Silu: "silu", AF.Ln: "exp"}


def row_of(t):
    return 0 if t < 4 else (1 if t < 8 else 2)


def build(debug_dump=None, upto=99):
    nc = bass.Bass("TRN2", target_bir_lowering=False)
    P = Prog()

    def din(name, shape, dt=F32):
        return nc.dram_tensor(name, list(shape), dt, kind="ExternalInput").ap()

    def dscr(name, shape, dt):
        return nc.dram_tensor(name, list(shape), dt, kind="Internal").ap()

    xT_in = din("xT_in", [8, 128, NT])
    cT_in = din("cT", [128, 8, 3])
    ada_w = din("ada_w", [2, 6, 128, 8, 1024])
    ada_b = din("ada_b", [2, 128, 48])
    n1g = din("n1g", [2, 128, 8])
    n2g = din("n2g", [2, 128, 8])
    fing = din("fing", [128, 8])
    w_out = din("w_out", [2, 128, 8, 1024])
    w1 = din("w1", [2, 128, 8, 4096])
    w2 = din("w2", [2, 128, 32, 1024])
    ev_w_in = din("ev_w_in", [128, 8, 1792])
    ev_gq = din("ev_gq", [128, 2, 64])
    ev_gk = din("ev_gk", [128, 2, 64])
    ev_cs = din("ev_cs", [128, 2, 17, 64])
    ev_sgug = din("ev_sgug", [128, 512])
    ev_wsT = din("ev_wsT", [128, 8, 128])
    ev_bsT = din("ev_bsT", [128, 8])
    od_w_in = din("od_w_in", [128, 8, 1440])
    od_gq = din("od_gq", [128, 256])
    od_gkv = din("od_gkv", [128, 128])
    od_wuq = din("od_wuq", [128, 2, 768])
    od_wukv = din("od_wukv", [128, 1024])
    od_cs = din("od_cs", [128, 2, 17, 32])
    od_cw = din("od_cw", [128, 4, 31])
    od_vec = din("od_vec", [128, 3, 4])
    outT = nc.dram_tensor("outT", [8, 128, NLAT], F32, kind="ExternalOutput").ap()

    xT = dscr("xT_s", [8, 128, NT], F32)
    oT = dscr("oT_s", [8, 128, NT], BF16)
    QT0 = dscr("QT0_s", [4, 128, NT], BF16)
    KT0 = dscr("KT0_s", [2, 128, NT], BF16)
    VA0 = dscr("VA0_s", [36, 128, 512], BF16)
    QT1 = dscr("QT1_s", [8, 96, NT], BF16)
    KT1 = dscr("KT1_s", [8, 96, NT], BF16)
    VA1 = dscr("VA1_s", [36, 128, 1024], BF16)
    YT = dscr("YT_s", [4, 128, NLAT], BF16)
    b_xT = [Buf() for _ in range(NTILE)]
    b_oTa = [Buf() for _ in range(NTILE)]
    b_oTb = [Buf() for _ in range(NTILE)]
    b_QT = [Buf() for _ in range(NTILE)]
    b_KT = [Buf() for _ in range(NTILE)]
    b_VA = [Buf() for _ in range(NTILE)]
    b_YT = [Buf() for _ in range(NTILE)]
    b_out = Buf()

    with ExitStack() as st:
        CAP = 200 * 1024
        arena_t = st.enter_context(nc.sbuf_tensor("arena", [128, CAP // 4], F32))
        A = Arena(arena_t, CAP)
        psbig = [st.enter_context(nc.psum_tensor("psw%d" % i, [128, 1024], F32)) for i in range(4)]
        psb = [T(psbig[i // 2][:, (i % 2) * 512:(i % 2 + 1) * 512], "ps%d" % i) for i in range(8)]
        for p_ in psb:
            p_.b.excl = True
        PS = Rot(psb[0:7])

        def ps_bf(p):
            return p.ap.bitcast(BF16)

        ident_f = A.f32(128, "identf")
        ident_b = A.bf16(128, "identb")
        ones_b = A.bf16(128, "onesb")
        ones_f = A.f32(128, "onesf")
        eps_t = A.f32(1, "eps")
        scT = A.f32(24, "scT")
        modT = A.f32(2 * 144, "modT")
        A1t = A.f32(2 * 24, "A1")
        A2t = A.f32(2 * 24, "A2")
        gn1 = A.f32(16, "gn1")
        gn2 = A.f32(16, "gn2")
        gfin = A.f32(8, "gfin")
        adab = A.f32(96, "adab")
        bconst = Buf("const")
        P.pool(lambda e: e.memset(ident_f.ap, 1.0), w=[bconst])
        P.pool(lambda e: e.affine_select(out=ident_f.ap, in_=ident_f.ap, pattern=[[-1, 128]],
                                         compare_op=ALU.is_equal, fill=0.0, base=0, channel_multiplier=1),
               r=[bconst], w=[bconst])
        P.dve(lambda e: e.tensor_copy(out=ident_b.ap, in_=ident_f.ap), r=[bconst], w=[bconst])
        P.pool(lambda e: e.memset(ones_b.ap, 1.0), w=[bconst])
        P.pool(lambda e: e.memset(ones_f.ap, 1.0), w=[bconst])
        P.pool(lambda e: e.memset(eps_t.ap, EPS), w=[bconst])
        P.dma("sp", lambda e: e.dma_start(out=scT.ap, in_=cT_in.rearrange("p c r -> p (c r)")), w=[scT.b])
        P.act(lambda e: e.activation(out=scT.ap, in_=scT.ap, func=AF.Silu), r=[scT.b], w=[scT.b])
        P.dma("sp", lambda e: e.dma_start(out=gn1.ap.rearrange("p (l c) -> p l c", l=2), in_=n1g.rearrange("l p c -> p l c")), w=[bconst])
        P.dma("sp", lambda e: e.dma_start(out=gn2.ap.rearrange("p (l c) -> p l c", l=2), in_=n2g.rearrange("l p c -> p l c")), w=[bconst])
        P.dma("sp", lambda e: e.dma_start(out=gfin.ap, in_=fing), w=[bconst])
        P.dma("sp", lambda e: e.dma_start(out=adab.ap.rearrange("p (l c) -> p l c", l=2), in_=ada_b.rearrange("l p c -> p l c")), w=[bconst])
        BASE = A.off

        mod3 = A1v = A2v = None

        def set_layer(L):
            nonlocal mod3, A1v, A2v
            mod3 = modT.ap[:, L * 144:(L + 1) * 144].rearrange("p (m r) -> p m r", r=3)
            A1v = A1t.ap[:, L * 24:(L + 1) * 24].rearrange("p (c r) -> p c r", r=3)
            A2v = A2t.ap[:, L * 24:(L + 1) * 24].rearrange("p (c r) -> p c r", r=3)
        set_layer(0)
        scT3 = scT.ap.rearrange("p (c r) -> p c r", r=3)

        def SH1(c, row):
            return mod3[:, 0 + c, row:row + 1]

        def G1(c, row):
            return mod3[:, 16 + c, row:row + 1]

        def SH2(c, row):
            return mod3[:, 24 + c, row:row + 1]

        def G2(c, row):
            return mod3[:, 40 + c, row:row + 1]

        def fsz(ap):
            n = 1
            for d in ap.shape[1:]:
                n *= int(d)
            return n

        def MM(out, lhsT, rhs, start, stop, r, w):
            n = fsz(rhs)
            c = 4.0 * max(n / 2.4, 107.0) if rhs.dtype == F32 else max(n / 2.4 + 5.0, 64.0)
            P.pe(lambda e: e.matmul(out, lhsT=lhsT, rhs=rhs, start=start, stop=stop), r, w, cost=c)

        def TR(out, in_, r, w):
            P.pe(lambda e: e.transpose(out=out, in_=in_, identity=ident_b.ap), list(r) + [bconst], w, cost=110.0)

        def ACT(out, in_, func, r, w, bias=None, scale=None):
            kw = {}
            if bias is not None:
                kw["bias"] = bias
            if scale is not None:
                kw["scale"] = scale
            i = P.act(lambda e: e.activation(out=out, in_=in_, func=func, **kw), r, w, cost=100.0 + fsz(out) / 1.2)
            P.ops[i].tbl = ACT_TBL.get(func)

        def ecost(eng, n, mult=1.0):
            return (70.0 + n / 0.96 * mult) if eng == "dve" else (100.0 + n / 0.55)

        def TT(eng, out, in0, in1, op, r, w):
            P.add(eng, lambda e: e.tensor_tensor(out=out, in0=in0, in1=in1, op=op), r, w, cost=ecost(eng, fsz(out)))

        def STT(out, in0, scalar, in1, op0, op1, r, w):
            P.dve(lambda e: e.scalar_tensor_tensor(out=out, in0=in0, scalar=scalar, in1=in1, op0=op0, op1=op1), r, w,
                  cost=ecost("dve", fsz(out)))

        def TS(eng, out, in0, scalar1, op0, r, w):
            P.add(eng, lambda e: e.tensor_scalar(out=out, in0=in0, scalar1=scalar1, scalar2=None, op0=op0), r, w,
                  cost=ecost(eng, fsz(out)))

        def RED(out, in_, r, w):
            P.dve(lambda e: e.tensor_reduce(out=out, in_=in_, axis=AX.X, op=ALU.add), r, w, cost=ecost("dve", fsz(in_)))

        def RECIP(out, in_, r, w):
            P.dve(lambda e: e.reciprocal(out=out, in_=in_), r, w, cost=ecost("dve", fsz(out), 8.0))

        def COPY(eng, out, in_, r, w):
            P.add(eng, lambda e: e.tensor_copy(out=out, in_=in_), r, w, cost=ecost(eng, fsz(out)))

        def DMA(q, out, in_, r, w, nobar=False):
            nbytes = 128 * fsz(out) * (2 if out.dtype == BF16 else 4)
            P.dma(q, lambda e: e.dma_start(out=out, in_=in_), r, w, cost=2000.0 + nbytes / 200.0, nobar=nobar)

        def MEMSET(ap, val, w):
            P.pool(lambda e: e.memset(ap, val), (), w, cost=ecost("pool", fsz(ap)))

        def cols(t):
            return slice(t * 512, (t + 1) * 512)

        def v3(ap, c):
            return ap.rearrange("p (c t) -> p c t", c=c)

        def load_x(xt, src, t, q="sp"):
            DMA(q, v3(xt.ap, 8), src[:, :, cols(t)].rearrange("c p t -> p c t"), [b_xT[t]], [xt.b])

        def store_x(xt, dst, t, dstbuf, q="sp"):
            DMA(q, dst[:, :, cols(t)].rearrange("c p t -> p c t"), v3(xt.ap, 8), [xt.b], [dstbuf])

        def rms_stats(xt, sq, rs, sqbufs=None, bank=None):
            sb_ = [sq.b] if sqbufs is None else sqbufs
            ACT(sq.ap, xt.ap, AF.Square, [xt.b], sb_)
            p = psb[7] if bank is None else bank
            for c in range(8):
                MM(p.ap, ones_b.ap, sq.ap[:, c * 512:(c + 1) * 512], c == 0, c == 7, sb_ + [bconst], [p.b])
            ACT(rs.ap, p.ap, AF.Sqrt, [p.b, bconst], [rs.b], bias=eps_t.ap[:, 0:1], scale=1.0 / 1024)
            RECIP(rs.ap, rs.ap, [rs.b], [rs.b])

        def modulate(xt, rs, Av, SHf, row, ht, tmps):
            for c in range(8):
                tm = tmps.next()
                STT(tm.ap, xt.ap[:, c * 512:(c + 1) * 512], Av[:, c, row:row + 1], rs.ap, ALU.mult, ALU.mult,
                    [xt.b, rs.b, modT.b], [tm.b])
                ACT(ht.ap[:, c * 512:(c + 1) * 512], tm.ap, AF.Identity, [tm.b, modT.b], [ht.b], bias=SHf(c, row), scale=1.0)

        def small_rstd(ss, n, inv_d):
            ACT(ss.ap[:, 0:n], ss.ap[:, 0:n], AF.Sqrt, [ss.b, bconst], [ss.b], bias=eps_t.ap[:, 0:1], scale=inv_d)
            RECIP(ss.ap[:, 0:n], ss.ap[:, 0:n], [ss.b], [ss.b])

        def cast_load(dst_ap, src_ap, wbuf, pieces=1):
            if pieces == 1:
                DMA("pool", dst_ap, src_ap, [], [wbuf])
                return
            n = dst_ap.shape[1]
            step = n // pieces
            for i in range(pieces):
                DMA("pool", dst_ap[:, i * step:(i + 1) * step], src_ap[:, i * step:(i + 1) * step], [], [wbuf])

        def mk(n, nm, f32=True, k=2):
            return Rot([(A.f32(n, nm + str(i)) if f32 else A.bf16(n, nm + str(i))) for i in range(k)])

        def rope(pT, pap, nh, hd, C3, S3, tabbufs, ta, tb, out_ap, outT_, eng_add="pool"):
            n = nh * hd
            hh = hd // 2
            p3 = pap.rearrange("p (h d) -> p h d", h=nh)
            TT("dve", ta.ap[:, 0:n].rearrange("p (h d) -> p h d", h=nh), p3, C3.unsqueeze(1).to_broadcast([128, nh, hd]),
               ALU.mult, [pT.b] + tabbufs, [ta.b])
            p4 = pap.rearrange("p (h i two) -> p h i two", h=nh, two=2)
            S4 = S3.rearrange("p (i two) -> p i two", two=2)
            tb4 = tb.ap[:, 0:n].rearrange("p (h i two) -> p h i two", h=nh, two=2)
            for a in range(2):
                TT("dve", tb4[:, :, :, a], p4[:, :, :, 1 - a], S4[:, :, a].unsqueeze(1).to_broadcast([128, nh, hh]),
                   ALU.mult, [pT.b] + tabbufs, [tb.b])
            TT(eng_add, out_ap, ta.ap[:, 0:n].rearrange("p (h d) -> p h d", h=nh),
               tb.ap[:, 0:n].rearrange("p (h d) -> p h d", h=nh), ALU.add, [ta.b, tb.b], [outT_.b])

        def phase_M(layers):
            A.off = BASE
            wts = [A.f32(8 * 1024, "adaw%d" % i) for i in range(2)]
            mrow = A.f32(6144, "mrow")
            k = 0
            for L in layers:
                set_layer(L)
                for pc in range(6):
                    wt = wts[k % 2]
                    k += 1
                    DMA("sp", wt.ap, ada_w[L, pc].rearrange("p c n -> p (c n)"), [], [wt.b])
                    for h in range(2):
                        p = PS.next()
                        for c in range(8):
                            MM(p.ap[0:3, :], scT3[:, c, :], wt.ap[:, c * 1024 + h * 512:c * 1024 + h * 512 + 512], c == 0, c == 7,
                               [wt.b, scT.b], [p.b])
                        COPY("dve", mrow.ap[0:3, (pc * 2 + h) * 512:(pc * 2 + h + 1) * 512], p.ap[0:3, :], [p.b], [mrow.b])
                pm = PS.next()
                for m in range(48):
                    P.pe(lambda e, m=m, pm=pm: e.transpose(out=pm.ap[:, m * 3:m * 3 + 3], in_=mrow.ap[0:3, m * 128:(m + 1) * 128],
                                                           identity=ident_f.ap[0:3, 0:3]), [mrow.b, bconst], [pm.b], cost=110.0)
                adab3 = adab.ap.rearrange("p (l c) -> p l c", l=2)
                TT("dve", mod3, pm.ap[:, 0:144].rearrange("p (m r) -> p m r", r=3),
                   adab3[:, L, :].unsqueeze(2).to_broadcast([128, 48, 3]), ALU.add, [pm.b, bconst], [modT.b])
                g1 = gn1.ap.rearrange("p (l c) -> p l c", l=2)[:, L, :].unsqueeze(2).to_broadcast([128, 8, 3])
                g2 = gn2.ap.rearrange("p (l c) -> p l c", l=2)[:, L, :].unsqueeze(2).to_broadcast([128, 8, 3])
                STT(A1v, mod3[:, 8:16, :], 1.0, g1, ALU.add, ALU.mult, [modT.b, bconst], [modT.b])
                STT(A2v, mod3[:, 32:40, :], 1.0, g2, ALU.add, ALU.mult, [modT.b, bconst], [modT.b])
            P.barrier()

        def phase_A0(xsrc):
            A.off = BASE
            w = A.bf16(8 * 1792, "w_in0")
            w3 = v3(w.ap, 8)
            cast_load(w3, ev_w_in, w.b, pieces=8)
            gq = A.f32(128, "gq")
            gk = A.f32(128, "gk")
            DMA("sp", gq.ap, ev_gq.rearrange("p a d -> p (a d)"), [], [gq.b])
            DMA("sp", gk.ap, ev_gk.rearrange("p a d -> p (a d)"), [], [gk.b])
            sgug = A.f32(512, "sgug")
            wsT = A.bf16(8 * 128, "wsT")
            bsT = A.f32(8, "bsT")
            DMA("sp", sgug.ap, ev_sgug, [], [sgug.b])
            cast_load(wsT.ap, ev_wsT.rearrange("p g q -> p (g q)"), wsT.b)
            DMA("sp", bsT.ap, ev_bsT, [], [bsT.b])
            tab = {}
            tabs = []
            for nm in ("q", "k"):
                tC = A.f32(17 * 64, "C" + nm)
                tS = A.f32(17 * 64, "S" + nm)
                tab[nm] = (tC, tS)
            xt = A.f32(4096, "x")
            cs4 = xt.ap[:, 0:2 * 17 * 64].rearrange("p (a b d) -> p a b d", a=2, b=17)
            DMA("sp", xt.ap[:, 0:2 * 17 * 64], ev_cs.rearrange("p a b d -> p (a b d)"), [], [xt.b])
            for nm, g in (("q", gq), ("k", gk)):
                tC, tS = tab[nm]
                g3 = g.ap.rearrange("p (a d) -> p a d", a=2)
                TT("dve", tC.ap.rearrange("p (b d) -> p b d", b=17), cs4[:, 0],
                   g3[:, 0, :].unsqueeze(1).to_broadcast([128, 17, 64]), ALU.mult, [xt.b, g.b], [tC.b])
                TT("dve", tS.ap.rearrange("p (b d) -> p b d", b=17), cs4[:, 1],
                   g3[:, 1, :].unsqueeze(1).to_broadcast([128, 17, 64]), ALU.mult, [xt.b, g.b], [tS.b])
            sq = A.bf16(4096, "sq")
            rs = A.f32(512, "rs")
            hts = Rot([A.bf16(4096, "h%d" % i) for i in range(2)])
            tmps = Rot([A.f32(512, "tm%d" % i) for i in range(3)])
            QTst = Rot([A.bf16(4 * 512, "QTst%d" % i) for i in range(2)])
            KTst = Rot([A.bf16(2 * 512, "KTst%d" % i) for i in range(2)])
            VAl = [A.bf16(4 * 512, "VAst%d" % i) for i in range(2)]
            for v in VAl:
                MEMSET(v.ap, 1.0, [v.b])
            VAst = Rot(VAl)
            OGst = Rot([A.bf16(4 * 512, "OGst%d" % i) for i in range(2)])
            sqq = mk(512, "sqq", k=1)
            ssq = mk(8, "ssq")
            t1 = mk(512, "t1", k=1)
            t2 = mk(512, "t2", k=1)
            qrot = mk(512, "qrot", f32=False)
            sqk = mk(128, "sqk")
            ssk = mk(8, "ssk")
            t1k = mk(128, "t1k")
            t2k = mk(128, "t2k")
            kdup = mk(256, "kdup", f32=False)
            uu = mk(512, "u")
            vg = mk(512, "vg")
            sqv = mk(512, "sqv", k=1)
            ssv = mk(8, "ssv")
            vnb = mk(512, "vnb", f32=False)
            tsv = mk(512, "tsv", k=1)
            og = mk(512, "og", f32=False)

            def rope_norm(pT, pap, nh, nm, blk, sqt, sst, ta, tb, outs):
                n = nh * 64
                C, S = tab[nm]
                ACT(sqt.ap[:, 0:n], pap, AF.Square, [pT.b], [sqt.b])
                RED(sst.ap[:, 0:nh], sqt.ap[:, 0:n].rearrange("p (h d) -> p h d", h=nh), [sqt.b], [sst.b])
                small_rstd(sst, nh, 1.0 / 64)
                C3 = C.ap.rearrange("p (b d) -> p b d", b=17)[:, blk, :]
                S3 = S.ap.rearrange("p (b d) -> p b d", b=17)[:, blk, :]
                rope(pT, pap, nh, 64, C3, S3, [C.b, S.b], ta, tb, ta.ap[:, 0:n].rearrange("p (h d) -> p h d", h=nh), ta)
                for (oT_, oap) in outs:
                    TT("dve", oap, ta.ap[:, 0:n].rearrange("p (h d) -> p h d", h=nh),
                       sst.ap[:, 0:nh].unsqueeze(2).to_broadcast([128, nh, 64]), ALU.mult, [ta.b, sst.b], [oT_.b])

            order = [8] + list(range(8))
            load_x(xt, xsrc, order[0])
            for ti, t in enumerate(order):
                row = row_of(t)
                rms_stats(xt, sq, rs)
                ht = hts.next()
                modulate(xt, rs, A1v, SH1, row, ht, tmps)
                if ti + 1 < len(order):
                    load_x(xt, xsrc, order[ti + 1])
                qst, kst, vst, ost = QTst.next(), KTst.next(), VAst.next(), OGst.next()
                for s in range(4):
                    blk = 16 if t == 8 else (t % 4) * 4 + s
                    sl = slice(s * 128, (s + 1) * 128)
                    pQ, pKV, pZU, pZV = PS.next(), PS.next(), PS.next(), PS.next()
                    groups = [(pQ, 0, 512), (pKV, 512, 256), (pZU, 768, 512), (pZV, 1280, 512)]
                    for c in range(8):
                        for (pp, c0, nn) in groups:
                            MM(pp.ap[:, 0:nn], ht.ap[:, c * 512 + s * 128:c * 512 + s * 128 + 128], w3[:, c, c0:c0 + nn],
                               c == 0, c == 7, [ht.b, w.b], [pp.b])
                    qr = qrot.next()
                    rope_norm(pQ, pQ.ap, 8, "q", blk, sqq.next(), ssq.next(), t1.next(), t2.next(),
                              [(qr, qr.ap.rearrange("p (h d) -> p h d", h=8))])
                    pT = PS.next()
                    pTb = ps_bf(pT)
                    for c in range(4):
                        TR(pTb[:, c * 128:(c + 1) * 128], qr.ap[:, c * 128:(c + 1) * 128], [qr.b], [pT.b])
                    ACT(v3(qst.ap, 4)[:, :, sl], v3(pTb[:, 0:512], 4), AF.Copy, [pT.b], [qst.b])
                    kd = kdup.next()
                    kd4 = kd.ap.rearrange("p (k d e) -> p k d e", k=2, d=2)
                    rope_norm(pKV, pKV.ap[:, 0:128], 2, "k", blk, sqk.next(), ssk.next(), t1k.next(), t2k.next(),
                              [(kd, kd4[:, :, 0, :]), (kd, kd4[:, :, 1, :])])
                    pT2 = PS.next()
                    pT2b = ps_bf(pT2)
                    for k in range(2):
                        TR(pT2b[:, k * 128:(k + 1) * 128], kd.ap[:, k * 128:(k + 1) * 128], [kd.b], [pT2.b])
                    ACT(v3(kst.ap, 2)[:, :, sl], v3(pT2b[:, 0:256], 2), AF.Copy, [pT2.b], [kst.b])
                    v5 = vst.ap.rearrange("p (s k a d) -> p s k a d", s=4, k=2, a=2)
                    vin = pKV.ap[:, 128:256].rearrange("p (k d) -> p k d", k=2)
                    ACT(v5[:, s, :, 0, 0:64], vin, AF.Copy, [pKV.b], [vst.b])
                    ACT(v5[:, s, :, 1, 64:128], vin, AF.Copy, [pKV.b], [vst.b])
                    u_, vg_, sqv_, ssv_, vnb_, tsv_, og_ = (uu.next(), vg.next(), sqv.next(), ssv.next(), vnb.next(),
                                                            tsv.next(), og.next())
                    ACT(u_.ap, pZU.ap, AF.Gelu_apprx_tanh, [pZU.b], [u_.b])
                    ACT(vg_.ap, pZV.ap, AF.Gelu_apprx_tanh, [pZV.b], [vg_.b])
                    TT("pool", sqv_.ap, vg_.ap, vg_.ap, ALU.mult, [vg_.b], [sqv_.b])
                    RED(ssv_.ap, sqv_.ap.rearrange("p (g d) -> p g d", g=8), [sqv_.b], [ssv_.b])
                    small_rstd(ssv_, 8, 1.0 / 64)
                    TT("dve", vg_.ap.rearrange("p (g d) -> p g d", g=8), vg_.ap.rearrange("p (g d) -> p g d", g=8),
                       ssv_.ap.unsqueeze(2).to_broadcast([128, 8, 64]), ALU.mult, [vg_.b, ssv_.b], [vg_.b])
                    TT("pool", vnb_.ap, vg_.ap, sgug.ap, ALU.mult, [vg_.b, sgug.b], [vnb_.b])
                    pS = PS.next()
                    for g in range(8):
                        MM(pS.ap[:, g * 64:(g + 1) * 64], wsT.ap[:, g * 128:(g + 1) * 128], vnb_.ap[:, g * 64:(g + 1) * 64],
                           True, True, [vnb_.b, wsT.b], [pS.b])
                    TT("dve", tsv_.ap.rearrange("p (g d) -> p g d", g=8), pS.ap.rearrange("p (g d) -> p g d", g=8),
                       bsT.ap.unsqueeze(2).to_broadcast([128, 8, 64]), ALU.add, [pS.b, bsT.b], [tsv_.b])
                    TT("pool", og_.ap, tsv_.ap, u_.ap, ALU.mult, [tsv_.b, u_.b], [og_.b])
                    pT3 = PS.next()
                    pT3b = ps_bf(pT3)
                    for c in range(4):
                        TR(pT3b[:, c * 128:(c + 1) * 128], og_.ap[:, c * 128:(c + 1) * 128], [og_.b], [pT3.b])
                    COPY("dve", v3(ost.ap, 4)[:, :, sl], v3(pT3b[:, 0:512], 4), [pT3.b], [ost.b])
                DMA("sp", QT0[:, :, cols(t)].rearrange("c p t -> p c t"), v3(qst.ap, 4), [qst.b], [b_QT[t]])
                DMA("sp", KT0[:, :, cols(t)].rearrange("c p t -> p c t"), v3(kst.ap, 2), [kst.b], [b_KT[t]])
                DMA("sp", VA0[t * 4:(t + 1) * 4].rearrange("s p f -> p s f"), v3(vst.ap, 4), [vst.b], [b_VA[t]])
                DMA("sp", oT[4:8, :, cols(t)].rearrange("c p t -> p c t"), v3(ost.ap, 4), [ost.b], [b_oTb[t]])
            P.barrier()

        def phase_B(layer, filler=None):
            A.off = BASE
            if layer == 0:
                KP, nkt, vw, scale = 128, 2, 512, 64 ** -0.5
                KTd, QTd, VAd = KT0, QT0, VA0
            else:
                KP, nkt, vw, scale = 96, 8, 1024, 96 ** -0.5
                KTd, QTd, VAd = KT1, QT1, VA1
            kt = A.bf16(nkt * 2304, "kt")
            kt3 = v3(kt.ap, nkt)
            va = A.bf16(18 * vw, "va")
            va3 = v3(va.ap, 18)
            qts = Rot([A.bf16(1024, "qt%d" % i) for i in range(3)])
            pp_ = Rot([A.bf16(1024, "Pp%d" % i) for i in range(3)])
            rec = Rot([A.f32(512, "rec%d" % i) for i in range(2)])
            ost = Rot([A.bf16(512, "ost%d" % i) for i in range(2)])
            Sbanks = Rot([(psb[0], psb[1], psbig[0]), (psb[2], psb[3], psbig[1])])
            if filler is None:
                Abanks = Rot([(psb[4], psb[5]), (psb[6], psb[7])])
            else:
                Abanks = Rot([(psb[4], psb[5])])
                cpn = Rot([A.f32(512, "cpn%d" % i) for i in range(2)])
                cpd = Rot([A.f32(512, "cpd%d" % i) for i in range(2)])
                PS.items = [psb[6], psb[7]]
                fst = filler[0]()
                if PREFETCH_D and layer == 0:
                    assert A.off <= W1_OFF, A.off
                    prefetch_D(0, which=("w1",))

            def load_q(job):
                (hp, q0, nq, nkb, tq) = job
                qt = qts.next()
                if layer == 0:
                    DMA("sp", qt.ap[:, 0:nq], QTd[hp, :, q0:q0 + nq], [b_QT[tq]], [qt.b])
                else:
                    DMA("sp", v3(qt.ap, 2)[0:96, :, 0:nq], QTd[2 * hp:2 * hp + 2, :, q0:q0 + nq].rearrange("h p t -> p h t"),
                        [b_QT[tq]], [qt.b])
                return qt

            def run_job(job, qt):
                (hp, q0, nq, nkb, tq) = job
                if layer == 0:
                    kv = hp // 2
                    q_e, q_o = qt.ap[0:64, 0:nq], qt.ap[64:128, 0:nq]
                    k_e = lambda kb: kt3[0:64, kv, kb * 128:(kb + 1) * 128]
                    k_o = lambda kb: kt3[64:128, kv, kb * 128:(kb + 1) * 128]
                    v_e = lambda kb: va3[:, kb, (2 * kv) * 128:(2 * kv + 1) * 128]
                    v_o = lambda kb: va3[:, kb, (2 * kv + 1) * 128:(2 * kv + 2) * 128]
                else:
                    q3 = v3(qt.ap, 2)
                    q_e, q_o = q3[0:96, 0, 0:nq], q3[0:96, 1, 0:nq]
                    k_e = lambda kb: kt3[0:96, 2 * hp, kb * 128:(kb + 1) * 128]
                    k_o = lambda kb: kt3[0:96, 2 * hp + 1, kb * 128:(kb + 1) * 128]
                    v_e = lambda kb: va3[:, kb, (2 * hp) * 128:(2 * hp + 1) * 128]
                    v_o = lambda kb: va3[:, kb, (2 * hp + 1) * 128:(2 * hp + 2) * 128]
                aE, aO = Abanks.next()

                def issue_S(kb):
                    sE, sO, sBig = Sbanks.next()
                    MM(sE.ap[:, 0:nq], k_e(kb), q_e, True, True, [kt.b, qt.b], [sE.b])
                    MM(sO.ap[:, 0:nq], k_o(kb), q_o, True, True, [kt.b, qt.b], [sO.b])
                    return sE, sO, sBig
                cur = issue_S(0)
                for kb in range(nkb):
                    nxt = issue_S(kb + 1) if kb + 1 < nkb else None
                    sE, sO, sBig = cur
                    pp = pp_.next()
                    if nq == 512:
                        ACT(pp.ap, sBig[:, 0:1024], AF.Exp, [sE.b, sO.b], [pp.b], scale=scale)
                    else:
                        ACT(v3(pp.ap, 2)[:, :, 0:nq], v3(sBig[:, 0:1024], 2)[:, :, 0:nq], AF.Exp, [sE.b, sO.b], [pp.b], scale=scale)
                    MM(aE.ap[:, 0:nq], v_e(kb), pp.ap[:, 0:nq], kb == 0, kb == nkb - 1, [va.b, pp.b], [aE.b])
                    MM(aO.ap[:, 0:nq], v_o(kb), pp.ap[:, 512:512 + nq], kb == 0, kb == nkb - 1, [va.b, pp.b], [aO.b])
                    cur = nxt
                rc, os_ = rec.next(), ost.next()
                if filler is None:
                    RECIP(rc.ap[64:128, 0:nq], aE.ap[64:128, 0:nq], [aE.b], [rc.b])
                    RECIP(rc.ap[0:64, 0:nq], aO.ap[0:64, 0:nq], [aO.b], [rc.b])
                    TT("dve", os_.ap[0:64, 0:nq], aE.ap[0:64, 0:nq], rc.ap[64:128, 0:nq], ALU.mult, [aE.b, rc.b], [os_.b])
                    TT("dve", os_.ap[64:128, 0:nq], aO.ap[64:128, 0:nq], rc.ap[0:64, 0:nq], ALU.mult, [aO.b, rc.b], [os_.b])
                else:
                    cn, cd = cpn.next(), cpd.next()
                    COPY("dve", cn.ap[0:64, 0:nq], aE.ap[0:64, 0:nq], [aE.b], [cn.b])
                    COPY("dve", cd.ap[0:64, 0:nq], aE.ap[64:128, 0:nq], [aE.b], [cd.b])
                    COPY("dve", cn.ap[64:128, 0:nq], aO.ap[64:128, 0:nq], [aO.b], [cn.b])
                    COPY("dve", cd.ap[64:128, 0:nq], aO.ap[0:64, 0:nq], [aO.b], [cd.b])
                    RECIP(cd.ap[:, 0:nq], cd.ap[:, 0:nq], [cd.b], [cd.b])
                    TT("dve", os_.ap[:, 0:nq], cn.ap[:, 0:nq], cd.ap[:, 0:nq], ALU.mult, [cn.b, cd.b], [os_.b])
                DMA("sp", oT[hp, :, q0:q0 + nq], os_.ap[:, 0:nq], [os_.b], [b_oTa[tq]])

            for b in range(2):
                tl = [4 * b + i for i in range(4)]
                DMA("sp", kt3[0:KP, :, 0:256], KTd[:, :, 4096 + 256 * b:4096 + 256 * (b + 1)].rearrange("k p t -> p k t"),
                    [b_KT[8]], [kt.b])
                DMA("sp", kt3[0:KP, :, 256:2304], KTd[:, :, 2048 * b:2048 * (b + 1)].rearrange("k p t -> p k t"),
                    [b_KT[i] for i in tl], [kt.b])
                DMA("sp", va3[:, 0:2, :], VAd[32 + 2 * b:34 + 2 * b].rearrange("s p f -> p s f"), [b_VA[8]], [va.b])
                DMA("sp", va3[:, 2:18, :], VAd[16 * b:16 * (b + 1)].rearrange("s p f -> p s f"), [b_VA[i] for i in tl], [va.b])
                jobs = []
                for qi in range(4):
                    for hp in range(4):
                        jobs.append((hp, 2048 * b + 512 * qi, 512, 18, 4 * b + qi))
                if layer == 0:
                    for hp in range(4):
                        jobs.append((hp, 4096 + 256 * b, 256, 2, 8))
                qcur = load_q(jobs[0])
                for ji, job in enumerate(jobs):
                    qnext = load_q(jobs[ji + 1]) if ji + 1 < len(jobs) else None
                    run_job(job, qcur)
                    qcur = qnext
                    if filler is not None and ji % 4 == 3 and ji < 16:
                        filler[1](fst, b, ji // 4)
            if filler is not None:
                filler[2](fst)
                PS.items = psb[0:7]
            P.barrier()

        W2_OFF, W1_OFF = 72 * 1024, 136 * 1024

        def d_weights():
            wa = T(arena_t[:, W1_OFF // 4:(W1_OFF + 65536) // 4].bitcast(BF16), "w1")
            wb = T(arena_t[:, W2_OFF // 4:(W2_OFF + 65536) // 4].bitcast(BF16), "w2")
            return wa, wb

        dw = {}

        def prefetch_D(L, which=("w1", "w2"), nobar=False):
            if L not in dw:
                wa, wb = d_weights()
                dw[L] = [wa, wb, set()]
            wa, wb, loaded = dw[L]
            wa3, wb3 = v3(wa.ap, 8), v3(wb.ap, 32)
            if "w1" in which and "w1" not in loaded:
                loaded.add("w1")
                for i in range(8):
                    DMA("pool", wa3[:, i:i + 1], w1[L][:, i:i + 1], [], [wa.b], nobar=nobar)
            if "w2" in which and "w2" not in loaded:
                loaded.add("w2")
                for i in range(8):
                    DMA("pool", wb3[:, 4 * i:4 * i + 4], w2[L][:, 4 * i:4 * i + 4], [], [wb.b], nobar=nobar)

        def c_alloc(L, limit=None):
            w = A.bf16(8 * 1024, "wout")
            cast_load(v3(w.ap, 8), w_out[L], w.b, pieces=4)
            xts = Rot([A.f32(4096, "x%d" % i) for i in range(2)])
            ots = Rot([A.bf16(4096, "o%d" % i) for i in range(2)])
            if limit is not None:
                assert A.off <= limit, A.off
            return (w, xts, ots)

        def c_tile(stt, L, xsrc, t):
            w, xts, ots = stt
            w3 = v3(w.ap, 8)
            row = row_of(t)
            xt, ot = xts.next(), ots.next()
            load_x(xt, xsrc, t)
            DMA("sp", v3(ot.ap, 8), oT[:, :, cols(t)].rearrange("c p t -> p c t"), [b_oTa[t], b_oTb[t]], [ot.b])
            for n in range(8):
                p = PS.next()
                for c in range(8):
                    MM(p.ap, w3[:, c, n * 128:(n + 1) * 128], ot.ap[:, c * 512:(c + 1) * 512], c == 0, c == 7, [w.b, ot.b], [p.b])
                STT(xt.ap[:, n * 512:(n + 1) * 512], p.ap, G1(n, row), xt.ap[:, n * 512:(n + 1) * 512], ALU.mult, ALU.add,
                    [p.b, xt.b, modT.b], [xt.b])
            store_x(xt, xT, t, b_xT[t])

        def phase_C(L, xsrc, tiles):
            A.off = BASE
            if PREFETCH_D:
                prefetch_D(L)
            stt = c_alloc(L, W2_OFF)
            for t in tiles:
                c_tile(stt, L, xsrc, t)
            P.barrier()

        def filler_C(L, xsrc):
            return (lambda: c_alloc(L),
                    lambda stt, b, qi: c_tile(stt, L, xsrc, 4 * b + qi),
                    lambda stt: c_tile(stt, L, xsrc, 8))

        def phase_D(L, tiles, final):
            A.off = BASE
            prefetch_D(L)
            wa, wb = dw[L][0], dw[L][1]
            wa3, wb3 = v3(wa.ap, 8), v3(wb.ap, 32)
            xts = Rot([A.f32(4096, "x%d" % i) for i in range(2)])
            rss = Rot([A.f32(512, "rs%d" % i) for i in range(2)])
            ht = A.bf16(4096, "h2")
            sq = T(ht.ap, "sq_alias")
            sq.b = ht.b
            hid0 = A.off
            hid = [A.bf16(512, "hid%d" % i) for i in range(16)]
            sqf = T(arena_t[:, hid0 // 4: hid0 // 4 + 2048].bitcast(BF16), "sqf")
            sqf_bufs = [h.b for h in hid[0:8]]
            tmps = Rot([A.f32(512, "tm%d" % i) for i in range(3)])
            assert A.off <= W2_OFF, A.off
            if final:
                PS.items = psb[0:6]
            for t in tiles:
                row = row_of(t)
                xt, rs = xts.next(), rss.next()
                load_x(xt, xT, t)
                rms_stats(xt, sq, rs)
                modulate(xt, rs, A2v, SH2, row, ht, tmps)
                for half in range(2):
                    for jj in range(16):
                        j = half * 16 + jj
                        p = PS.next()
                        for c in range(8):
                            MM(p.ap, wa3[:, c, j * 128:(j + 1) * 128], ht.ap[:, c * 512:(c + 1) * 512], c == 0, c == 7,
                               [wa.b, ht.b], [p.b])
                        tm = tmps.next()
                        ACT(tm.ap, p.ap, AF.Relu, [p.b], [tm.b])
                        TT("pool", hid[jj].ap, tm.ap, tm.ap, ALU.mult, [tm.b], [hid[jj].b])
                    for n in range(8):
                        p = PS.next()
                        for jj in range(16):
                            j = half * 16 + jj
                            MM(p.ap, wb3[:, j, n * 128:(n + 1) * 128], hid[jj].ap, jj == 0, jj == 15, [wb.b, hid[jj].b], [p.b])
                        STT(xt.ap[:, n * 512:(n + 1) * 512], p.ap, G2(n, row), xt.ap[:, n * 512:(n + 1) * 512], ALU.mult, ALU.add,
                            [p.b, xt.b, modT.b], [xt.b])
                if not final:
                    store_x(xt, xT, t, b_xT[t])
                else:
                    rms_stats(xt, sqf, rs, sqbufs=sqf_bufs, bank=psb[6])
                    for c in range(8):
                        STT(xt.ap[:, c * 512:(c + 1) * 512], xt.ap[:, c * 512:(c + 1) * 512], gfin.ap[:, c:c + 1], rs.ap,
                            ALU.mult, ALU.mult, [xt.b, rs.b, bconst], [xt.b])
                    store_x(xt, outT, t, b_out)
            PS.items = psb[0:7]
            P.barrier()

        def phase_A1():
            A.off = BASE
            w = A.bf16(8 * 1440, "w_in1")
            w3 = v3(w.ap, 8)
            cast_load(w3, od_w_in, w.b, pieces=8)
            wuq = A.bf16(2 * 768, "wuq")
            wuq3 = v3(wuq.ap, 2)
            cast_load(wuq.ap, od_wuq.rearrange("p c n -> p (c n)"), wuq.b)
            wukv = A.bf16(1024, "wukv")
            cast_load(wukv.ap, od_wukv, wukv.b)
            gq = A.f32(256, "gq1")
            gkv = A.f32(128, "gkv1")
            cs = A.f32(2 * 17 * 32, "cs1")
            DMA("sp", gq.ap, od_gq, [], [gq.b])
            DMA("sp", gkv.ap, od_gkv, [], [gkv.b])
            DMA("sp", cs.ap, od_cs.rearrange("p a b d -> p (a b d)"), [], [cs.b])
            cs4 = cs.ap.rearrange("p (a b d) -> p a b d", a=2, b=17)

            xt = A.f32(4096, "x")
            sq = A.bf16(4096, "sq")
            rs = A.f32(512, "rs")
            hts = Rot([A.bf16(4096, "h%d" % i) for i in range(2)])
            tmps = Rot([A.f32(512, "tm%d" % i) for i in range(3)])
            QTst = Rot([A.bf16(8 * 512, "QTst%d" % i) for i in range(2)])
            KTst = Rot([A.bf16(8 * 512, "KTst%d" % i) for i in range(2)])
            VAl = [A.bf16(4 * 1024, "VAst%d" % i) for i in range(2)]
            for v in VAl:
                MEMSET(v.ap, 1.0, [v.b])
            VAst = Rot(VAl)
            Yst = Rot([A.bf16(4 * 512, "Yst%d" % i) for i in range(2)])
            sqp = mk(384, "sqp")
            ss2 = mk(2, "ss2")
            cqn = mk(256, "cqn", f32=False)
            ckvn = mk(128, "ckvn", f32=False)
            cqT = mk(256, "cqT", f32=False)
            ckT = mk(128, "ckT", f32=False)
            kt1 = mk(32, "kt1")
            kt2 = mk(32, "kt2")
            qa = mk(768, "qa", f32=False)
            ka = mk(768, "ka", f32=False)
            qt1 = mk(256, "qt1")
            qt2 = mk(256, "qt2")
            sg = mk(512, "sg")

            order = [8] + list(range(8))
            load_x(xt, xT, order[0])
            for ti, t in enumerate(order):
                lat = t < 8
                row = row_of(t)
                rms_stats(xt, sq, rs)
                ht = hts.next()
                modulate(xt, rs, A1v, SH1, row, ht, tmps)
                if ti + 1 < len(order):
                    load_x(xt, xT, order[ti + 1])
                qst, kst, vst, yst = QTst.next(), KTst.next(), VAst.next(), Yst.next()
                for s in range(4):
                    blk = 16 if t == 8 else (t % 4) * 4 + s
                    sl = slice(s * 128, (s + 1) * 128)
                    C3 = cs4[:, 0, blk, :]
                    S3 = cs4[:, 1, blk, :]
                    pP = PS.next()
                    for c in range(8):
                        MM(pP.ap[:, 0:416], ht.ap[:, c * 512 + s * 128:c * 512 + s * 128 + 128], w3[:, c, 0:416], c == 0, c == 7,
                           [ht.b, w.b], [pP.b])
                    sqp_, ss_ = sqp.next(), ss2.next()
                    ACT(sqp_.ap, pP.ap[:, 0:384], AF.Square, [pP.b], [sqp_.b])
                    RED(ss_.ap[:, 0:1], sqp_.ap[:, 0:256], [sqp_.b], [ss_.b])
                    RED(ss_.ap[:, 1:2], sqp_.ap[:, 256:384], [sqp_.b], [ss_.b])
                    ACT(ss_.ap[:, 0:1], ss_.ap[:, 0:1], AF.Sqrt, [ss_.b, bconst], [ss_.b], bias=eps_t.ap[:, 0:1], scale=1.0 / 256)
                    ACT(ss_.ap[:, 1:2], ss_.ap[:, 1:2], AF.Sqrt, [ss_.b, bconst], [ss_.b], bias=eps_t.ap[:, 0:1], scale=1.0 / 128)
                    RECIP(ss_.ap, ss_.ap, [ss_.b], [ss_.b])
                    ckvn_, ckT_ = ckvn.next(), ckT.next()
                    STT(ckvn_.ap, pP.ap[:, 256:384], ss_.ap[:, 1:2], gkv.ap, ALU.mult, ALU.mult, [pP.b, ss_.b, gkv.b], [ckvn_.b])
                    pT = PS.next()
                    pTb = ps_bf(pT)
                    TR(pTb[:, 0:128], ckvn_.ap, [ckvn_.b], [pT.b])
                    cqT_ = None
                    if lat:
                        cqn_, cqT_ = cqn.next(), cqT.next()
                        STT(cqn_.ap, pP.ap[:, 0:256], ss_.ap[:, 0:1], gq.ap, ALU.mult, ALU.mult, [pP.b, ss_.b, gq.b], [cqn_.b])
                        for k in range(2):
                            TR(pTb[:, 128 + k * 128:256 + k * 128], cqn_.ap[:, k * 128:(k + 1) * 128], [cqn_.b], [pT.b])
                        ACT(cqT_.ap, pTb[:, 128:384], AF.Copy, [pT.b], [cqT_.b])
                    ACT(ckT_.ap, pTb[:, 0:128], AF.Copy, [pT.b], [ckT_.b])
                    ka_ = ka.next()
                    ka3 = ka_.ap.rearrange("p (h d) -> p h d", h=8)
                    pK = PS.next()
                    MM(pK.ap, ckT_.ap, wukv.ap[:, 0:512], True, True, [ckT_.b, wukv.b], [pK.b])
                    ACT(ka3[:, :, 0:64], pK.ap.rearrange("p (h d) -> p h d", h=8), AF.Copy, [pK.b], [ka_.b])
                    kt1_, kt2_ = kt1.next(), kt2.next()
                    rope(pP, pP.ap[:, 384:416], 1, 32, C3, S3, [cs.b], kt1_, kt2_, kt1_.ap.rearrange("p (h d) -> p h d", h=1), kt1_)
                    COPY("dve", ka3[:, :, 64:96], kt1_.ap.unsqueeze(1).to_broadcast([128, 8, 32]), [kt1_.b], [ka_.b])
                    pTK = PS.next()
                    pTKb = ps_bf(pTK)
                    for h in range(8):
                        TR(pTKb[0:96, h * 128:(h + 1) * 128], ka3[:, h, :], [ka_.b], [pTK.b])
                    COPY("dve", v3(kst.ap, 8)[0:96, :, sl], v3(pTKb[0:96, :], 8), [pTK.b], [kst.b])
                    pV = PS.next()
                    MM(pV.ap, ckT_.ap, wukv.ap[:, 512:1024], True, True, [ckT_.b, wukv.b], [pV.b])
                    v5 = vst.ap.rearrange("p (s h a d) -> p s h a d", s=4, h=4, a=2)
                    pV4 = pV.ap.rearrange("p (h a d) -> p h a d", h=4, a=2)
                    ACT(v5[:, s, :, 0, 0:64], pV4[:, :, 0, :], AF.Copy, [pV.b], [vst.b])
                    ACT(v5[:, s, :, 1, 64:128], pV4[:, :, 1, :], AF.Copy, [pV.b], [vst.b])
                    if lat:
                        qa_ = qa.next()
                        qa3 = qa_.ap.rearrange("p (h d) -> p h d", h=8)
                        pQ1, pQ2 = PS.next(), PS.next()
                        for k in range(2):
                            MM(pQ1.ap, cqT_.ap[:, k * 128:(k + 1) * 128], wuq3[:, k, 0:512], k == 0, k == 1, [cqT_.b, wuq.b], [pQ1.b])
                        for k in range(2):
                            MM(pQ2.ap[:, 0:256], cqT_.ap[:, k * 128:(k + 1) * 128], wuq3[:, k, 512:768], k == 0, k == 1,
                               [cqT_.b, wuq.b], [pQ2.b])
                        ACT(qa3[:, :, 0:64], pQ1.ap.rearrange("p (h d) -> p h d", h=8), AF.Copy, [pQ1.b], [qa_.b])
                        rope(pQ2, pQ2.ap[:, 0:256], 8, 32, C3, S3, [cs.b], qt1.next(), qt2.next(), qa3[:, :, 64:96], qa_)
                        pTQ = PS.next()
                        pTQb = ps_bf(pTQ)
                        for h in range(8):
                            TR(pTQb[0:96, h * 128:(h + 1) * 128], qa3[:, h, :], [qa_.b], [pTQ.b])
                        COPY("dve", v3(qst.ap, 8)[0:96, :, sl], v3(pTQb[0:96, :], 8), [pTQ.b], [qst.b])
                if lat:
                    for j in range(4):
                        pa, pg = PS.next(), PS.next()
                        for (pp, c0) in ((pa, 416 + j * 128), (pg, 416 + 512 + j * 128)):
                            for c in range(8):
                                MM(pp.ap, w3[:, c, c0:c0 + 128], ht.ap[:, c * 512:(c + 1) * 512], c == 0, c == 7, [w.b, ht.b], [pp.b])
                        sg_ = sg.next()
                        ACT(sg_.ap, pg.ap, AF.Sigmoid, [pg.b], [sg_.b])
                        TT("dve", yst.ap[:, j * 512:(j + 1) * 512], pa.ap, sg_.ap, ALU.mult, [pa.b, sg_.b], [yst.b])
                    DMA("sp", QT1[:, :, cols(t)].rearrange("h p t -> p h t"), v3(qst.ap, 8)[0:96], [qst.b], [b_QT[t]])
                    DMA("sp", YT[:, :, cols(t)].rearrange("c p t -> p c t"), v3(yst.ap, 4), [yst.b], [b_YT[t]])
                DMA("sp", KT1[:, :, cols(t)].rearrange("h p t -> p h t"), v3(kst.ap, 8)[0:96], [kst.b], [b_KT[t]])
                DMA("sp", VA1[t * 4:(t + 1) * 4].rearrange("s p f -> p s f"), v3(vst.ap, 4), [vst.b], [b_VA[t]])
            P.barrier()

        def e_alloc():
            cw = A.f32(4 * 31, "cw")
            vec = A.f32(12, "vec")
            DMA("sp", cw.ap, od_cw.rearrange("p j k -> p (j k)"), [], [cw.b])
            DMA("sp", vec.ap, od_vec.rearrange("p a j -> p (a j)"), [], [vec.b])
            D = A.bf16(4 * 31 * 128, "D")
            Db = [Buf() for _ in range(124)]
            for i in range(124):
                TS("dve" if i % 3 != 2 else "pool", D.ap[:, i * 128:(i + 1) * 128], ident_f.ap, cw.ap[:, i:i + 1], ALU.mult,
                   [cw.b, bconst], [Db[i]])
            ypad = A.bf16(4 * 2080, "ypad")
            MEMSET(ypad.ap, 0.0, [ypad.b])
            yc = A.f32(4 * 512, "yc")
            ysq = A.f32(4 * 512, "ysq")
            mu = A.f32(512, "mu")
            msq = A.f32(512, "msq")
            rsd = A.f32(512, "rsd")
            tms = Rot([A.f32(512, "tl%d" % i) for i in range(2)])
            ost = Rot([A.bf16(4 * 512, "ost%d" % i) for i in range(2)])
            return (vec, D, Db, ypad, yc, ysq, mu, msq, rsd, tms, ost)

        def e_tile(stt, b, qi):
            (vec, D, Db, ypad, yc, ysq, mu, msq, rsd, tms, ost) = stt
            yp3 = v3(ypad.ap, 4)
            if qi == 0:
                DMA("sp", yp3[:, :, 15:15 + 2048], YT[:, :, 2048 * b:2048 * (b + 1)].rearrange("c p t -> p c t"),
                    [b_YT[4 * b + i] for i in range(4)], [ypad.b])
            t = 4 * b + qi
            t0 = qi * 512
            for j in range(4):
                p = PS.next()
                for k in range(31):
                    MM(p.ap, D.ap[:, (j * 31 + k) * 128:(j * 31 + k + 1) * 128], yp3[:, j, t0 + k:t0 + k + 512],
                       k == 0, k == 30, [Db[j * 31 + k], ypad.b], [p.b])
                TS("dve", yc.ap[:, j * 512:(j + 1) * 512], p.ap, vec.ap[:, j:j + 1], ALU.add, [p.b, vec.b], [yc.b])
            TT("pool", ysq.ap, yc.ap, yc.ap, ALU.mult, [yc.b], [ysq.b])
            pm, pq = PS.next(), PS.next()
            for j in range(4):
                MM(pm.ap, ones_f.ap, yc.ap[:, j * 512:(j + 1) * 512], j == 0, j == 3, [yc.b, bconst], [pm.b])
            TS("dve", mu.ap, pm.ap, 1.0 / 512, ALU.mult, [pm.b], [mu.b])
            for j in range(4):
                MM(pq.ap, ones_f.ap, ysq.ap[:, j * 512:(j + 1) * 512], j == 0, j == 3, [ysq.b, bconst], [pq.b])
            TT("pool", msq.ap, mu.ap, mu.ap, ALU.mult, [mu.b], [msq.b])
            STT(rsd.ap, pq.ap, 1.0 / 512, msq.ap, ALU.mult, ALU.subtract, [pq.b, msq.b], [rsd.b])
            ACT(rsd.ap, rsd.ap, AF.Sqrt, [rsd.b, bconst], [rsd.b], bias=eps_t.ap[:, 0:1], scale=1.0)
            RECIP(rsd.ap, rsd.ap, [rsd.b], [rsd.b])
            os_ = ost.next()
            for j in range(4):
                tm = tms.next()
                TT("dve", tm.ap, yc.ap[:, j * 512:(j + 1) * 512], mu.ap, ALU.subtract, [yc.b, mu.b], [tm.b])
                TT("pool", tm.ap, tm.ap, rsd.ap, ALU.mult, [tm.b, rsd.b], [tm.b])
                ACT(os_.ap[:, j * 512:(j + 1) * 512], tm.ap, AF.Silu, [tm.b, vec.b], [os_.b],
                    bias=vec.ap[:, 8 + j:9 + j], scale=vec.ap[:, 4 + j:5 + j])
            DMA("sp", oT[4:8, :, cols(t)].rearrange("c p t -> p c t"), v3(os_.ap, 4), [os_.b], [b_oTb[t]])

        def phase_E1():
            A.off = BASE
            stt = e_alloc()
            for b in range(2):
                for qi in range(4):
                    e_tile(stt, b, qi)
            P.barrier()

        def filler_E1():
            return (e_alloc, e_tile, lambda stt: None)

        P.barrier()
        if upto >= 0:
            phase_M([0, 1])
            set_layer(0)
        if upto >= 1:
            phase_A0(xT_in)
        if upto >= 2:
            if MERGE_FILL:
                phase_B(0, filler_C(0, xT_in))
            else:
                phase_B(0)
        if upto >= 3 and not MERGE_FILL:
            phase_C(0, xT_in, list(range(9)))
        if upto >= 4:
            phase_D(0, list(range(9)), final=False)
        if upto >= 5:
            set_layer(1)
            phase_A1()
        if upto >= 6:
            if MERGE_FILL:
                phase_B(1, filler_E1())
            else:
                phase_B(1)
        if upto >= 7 and not MERGE_FILL:
            phase_E1()
        if upto >= 8:
            phase_C(1, xT, list(range(8)))
        if upto >= 9:
            phase_D(1, list(range(8)), final=True)
        dbg = []
        if debug_dump is not None:
            dbg = debug_dump(nc, P, locals())
        P.add("sp", lambda e: e.nop(), [b_out] + list(dbg))
        stats = P.emit(nc, st)
    return nc, stats


def _rope_tables(d_rot):
    GRID_W = 64
    pos = np.arange(2048)
    row = (pos // GRID_W).astype(np.float64)
    col = (pos % GRID_W).astype(np.float64)
    d_axis = d_rot // 2
    inv = 10000.0 ** (-np.arange(0, d_axis, 2, dtype=np.float64) / d_axis)
    ang = np.concatenate([row[:, None] * inv, col[:, None] * inv], axis=-1).astype(np.float32)
    ang = ang.astype(np.float64)
    cos, sin = np.cos(ang), np.sin(ang)
    cos2 = np.repeat(cos, 2, axis=1)
    sins = np.stack([-sin, sin], axis=-1).reshape(2048, d_rot)
    out = np.zeros((128, 2, 17, d_rot), np.float32)
    out[:, 0, :16] = cos2.reshape(16, 128, d_rot).transpose(1, 0, 2)
    out[:, 1, :16] = sins.reshape(16, 128, d_rot).transpose(1, 0, 2)
    out[:, 0, 16] = 1.0
    out[:, 1, 16] = 0.0
    return out


def _pairswap(g):
    return g.reshape(-1, 2)[:, ::-1].reshape(-1)


def _rep(v):
    return np.ascontiguousarray(np.broadcast_to(np.asarray(v, np.float32).reshape(1, -1), (128, v.size)))


def _kc(w):
    K, N = w.shape
    return np.ascontiguousarray(w.reshape(K // 128, 128, N).transpose(1, 0, 2))


def _vecT(v):
    return np.ascontiguousarray(v.reshape(-1, 128).T)


def prep_shared(inp):
    f = lambda a: np.asarray(a, np.float32)
    sh = {}
    ada_w = f(inp["ada_w"])
    sh["ada_w"] = np.ascontiguousarray(ada_w.reshape(2, 8, 128, 6, 1024).transpose(0, 3, 2, 1, 4))
    sh["ada_b"] = np.ascontiguousarray(f(inp["ada_b"]).reshape(2, 48, 128).transpose(0, 2, 1))
    sh["n1g"] = np.ascontiguousarray(f(inp["norm1_g"]).reshape(2, 8, 128).transpose(0, 2, 1))
    sh["n2g"] = np.ascontiguousarray(f(inp["norm2_g"]).reshape(2, 8, 128).transpose(0, 2, 1))
    sh["fing"] = _vecT(f(inp["final_g"]))
    sh["w_out"] = np.stack([_kc(f(inp["w_out"])[i]) for i in range(2)])
    sh["w1"] = np.stack([_kc(f(inp["mlp_w1"])[i]) for i in range(2)])
    sh["w2"] = np.stack([_kc(f(inp["mlp_w2"])[i]) for i in range(2)])
    sh["ev_w_in"] = _kc(f(inp["ev_w_in"])[0])
    gq = f(inp["ev_q_norm_g"])[0]
    gk = f(inp["ev_k_norm_g"])[0]
    sh["ev_gq"] = np.ascontiguousarray(np.stack([_rep(gq), _rep(_pairswap(gq))], axis=1))
    sh["ev_gk"] = np.ascontiguousarray(np.stack([_rep(gk), _rep(_pairswap(gk))], axis=1))
    sh["ev_cs"] = _rope_tables(64)
    sh["ev_sgug"] = _rep(f(inp["ev_sgu_norm_g"])[0].reshape(-1))
    sh["ev_wsT"] = np.ascontiguousarray(f(inp["ev_sgu_w"])[0].transpose(2, 0, 1))
    sh["ev_bsT"] = np.ascontiguousarray(f(inp["ev_sgu_b"])[0].T)
    sh["od_w_in"] = _kc(f(inp["od_w_in"])[0])
    sh["od_gq"] = _rep(f(inp["od_q_norm_g"])[0])
    sh["od_gkv"] = _rep(f(inp["od_kv_norm_g"])[0])
    wuq = f(inp["od_w_uq"])[0].reshape(256, 8, 96)
    wuq = np.concatenate([wuq[:, :, :64].reshape(256, 512), wuq[:, :, 64:].reshape(256, 256)], axis=1)
    sh["od_wuq"] = _kc(wuq)
    wukv = f(inp["od_w_ukv"])[0].reshape(128, 8, 128)
    sh["od_wukv"] = np.ascontiguousarray(np.concatenate([wukv[:, :, :64].reshape(128, 512), wukv[:, :, 64:].reshape(128, 512)], axis=1))
    sh["od_cs"] = _rope_tables(32)
    sh["od_cw"] = np.ascontiguousarray(f(inp["od_conv_w"])[0].T.reshape(4, 128, 31).transpose(1, 0, 2))
    sh["od_vec"] = np.ascontiguousarray(np.stack([_vecT(f(inp["od_conv_b"])[0]), _vecT(f(inp["od_ln_g"])[0]),
                                                  _vecT(f(inp["od_ln_b"])[0])], axis=1))
    return sh


def prep_core(inp, k):
    x = np.asarray(inp["x"], np.float32)
    ctx = np.asarray(inp["ctx"], np.float32)
    c = np.asarray(inp["c"], np.float32)
    cc = np.asarray(inp["c_ctx"], np.float32)
    b0, b1 = 2 * k, 2 * k + 1
    tok = np.concatenate([x[b0], x[b1], ctx[b0], ctx[b1]], axis=0)
    xT = np.ascontiguousarray(tok.T).reshape(8, 128, NT)
    rows = np.stack([c[b0], c[b1], cc], axis=0)
    cT = np.ascontiguousarray(rows.reshape(3, 8, 128).transpose(2, 1, 0))
    return {"xT_in": xT, "cT": cT}


_CACHE = {}


def kernel(**inputs):
    if "nc" not in _CACHE:
        _CACHE["nc"] = build()[0]
    nc = _CACHE["nc"]
    sh = prep_shared(inputs)
    in_maps = []
    for k in range(8):
        m = dict(sh)
        m.update(prep_core(inputs, k))
        in_maps.append(m)
    res = run_bass_kernel_spmd(nc, in_maps, core_ids=list(range(8)))
    out = np.empty((16, 2048, 1024), np.float32)
    for k in range(8):
        o = np.asarray(res.results[k]["outT"]).reshape(1024, NLAT)
        out[2 * k] = o[:, 0:2048].T
        out[2 * k + 1] = o[:, 2048:4096].T
    return out
```
